# Optimizing a Trainium2 kernel written in Bass

```python
import math
import jax, jax.numpy as jnp
from jax import lax
import numpy as np

D_MODEL = 1024
BATCH = 32
SEQ = 2048
DEPTH = 1

HEAD_DIM = 64
DA_HEADS = 8
DA_V_DIM = 2 * HEAD_DIM
NSA_HEADS = 16
NSA_GROUPS = 2
NSA_HPG = NSA_HEADS // NSA_GROUPS
CMP_BLOCK = 32
CMP_STRIDE = 16
CMP_HIDDEN = 256
SLC_BLOCK = 64
SLC_TOPN = 16
WIN = 512
Q_BLOCK = 128
SLC_Q_BLOCK = 64
D_FF = 2816
ROPE_THETA = 10000.0
EPS = 1e-6
NEG_INF = -1e30
FORCE_SCORE = 1e9

DA_QK_W = DA_HEADS * 2 * HEAD_DIM
DA_V_W = DA_HEADS * DA_V_DIM
NSA_Q_W = NSA_HEADS * HEAD_DIM
NSA_KV_W = NSA_GROUPS * HEAD_DIM
NSA_GATE_W = 3 * NSA_HEADS
IN_WIDTHS = (DA_QK_W, DA_QK_W, DA_V_W, NSA_Q_W) + (NSA_KV_W,) * 6 + (NSA_GATE_W, D_MODEL, D_MODEL)
D_IN = sum(IN_WIDTHS)

kernel_name = 'hybrid_diffattn_nsa_macaron'


def rmsnorm(x, g):
    xf = x.astype(jnp.float32)
    y = xf * lax.rsqrt(jnp.mean(xf * xf, axis=-1, keepdims=True) + EPS)
    return (y * g.astype(jnp.float32)).astype(x.dtype)


def swiglu(h, w1, w3, w2):
    return (jax.nn.silu(h @ w1) * (h @ w3)) @ w2


def rope_tables(T, dim):
    pos = jnp.arange(T, dtype=jnp.float32)
    inv = 1.0 / (ROPE_THETA ** (jnp.arange(0, dim, 2, dtype=jnp.float32) / dim))
    ang = pos[:, None] * inv[None, :]
    return jnp.cos(ang), jnp.sin(ang)


def apply_rope(x, cos, sin):
    half = x.shape[-1] // 2
    shape = (1, x.shape[1]) + (1,) * (x.ndim - 3) + (half,)
    c = cos.reshape(shape).astype(x.dtype)
    s = sin.reshape(shape).astype(x.dtype)
    x1, x2 = x[..., :half], x[..., half:]
    return jnp.concatenate([x1 * c - x2 * s, x2 * c + x1 * s], axis=-1)


def _sweep(block_fn, n_blocks):
    out = lax.map(block_fn, jnp.arange(n_blocks))
    out = jnp.moveaxis(out, 0, 1)
    return out.reshape((out.shape[0], n_blocks * out.shape[2]) + out.shape[3:])


def diff_attention(q, k, v, cos, sin, lam_params, head_gain, lam_init):
    B, T = q.shape[0], q.shape[1]
    q = apply_rope(q, cos, sin)
    k = apply_rope(k, cos, sin)
    lp = lam_params.astype(jnp.float32)
    lam = jnp.exp(jnp.sum(lp[0] * lp[1])) - jnp.exp(jnp.sum(lp[2] * lp[3])) + lam_init
    kpos = jnp.arange(T)
    scale = HEAD_DIM ** -0.5

    def block(i):
        qb = lax.dynamic_slice_in_dim(q, i * Q_BLOCK, Q_BLOCK, axis=1)
        qpos = i * Q_BLOCK + jnp.arange(Q_BLOCK)
        s = jnp.einsum('bqhcd,bkhcd->bhcqk', qb, k).astype(jnp.float32) * scale
        s = jnp.where(kpos[None, :] <= qpos[:, None], s, NEG_INF)
        p = jax.nn.softmax(s, axis=-1)
        a = p[:, :, 0] - lam * p[:, :, 1]
        return jnp.einsum('bhqk,bkhe->bqhe', a.astype(v.dtype), v)

    o = _sweep(block, T // Q_BLOCK)
    o = rmsnorm(o, head_gain) * (1.0 - lam_init)
    return o.reshape(B, T, DA_V_W)


def compress(x_raw, pos, w1, w2):
    B, T = x_raw.shape[0], x_raw.shape[1]
    n_cmp = (T - CMP_BLOCK) // CMP_STRIDE + 1
    idx = np.arange(n_cmp)[:, None] * CMP_STRIDE + np.arange(CMP_BLOCK)[None, :]
    blocks = x_raw[:, idx] + pos[:, None, :]
    blocks = jnp.swapaxes(blocks, 2, 3).reshape(B, n_cmp, NSA_GROUPS, CMP_BLOCK * HEAD_DIM)
    return jax.nn.gelu(blocks @ w1) @ w2


def _overlap_matrix(n_cmp, n_slc):
    start = np.arange(n_cmp) * CMP_STRIDE
    sb = np.arange(n_slc) * SLC_BLOCK
    ov = (start[:, None] < sb[None, :] + SLC_BLOCK) & (start[:, None] + CMP_BLOCK > sb[None, :])
    return ov.astype(np.float32)


def nsa(q, k_cmp, v_cmp, k_slc, v_slc, k_win, v_win, gate_logits, cos, sin, cmp_pos, cmp_w1, cmp_w2):
    B, T = q.shape[0], q.shape[1]
    dt = q.dtype
    scale = HEAD_DIM ** -0.5
    t = jnp.arange(T)

    kc = compress(k_cmp, cmp_pos[0], cmp_w1[0], cmp_w2[0])
    vc = compress(v_cmp, cmp_pos[1], cmp_w1[1], cmp_w2[1])
    n_cmp = kc.shape[1]
    s = jnp.einsum('btgjd,bngd->btgjn', q, kc).astype(jnp.float32) * scale
    c_valid = ((jnp.arange(n_cmp) * CMP_STRIDE + CMP_BLOCK - 1)[None, :] <= t[:, None])[None, :, None, None, :]
    p_cmp = jnp.where(c_valid, jax.nn.softmax(jnp.where(c_valid, s, NEG_INF), axis=-1), 0.0)
    o_cmp = jnp.einsum('btgjn,bngd->btgjd', p_cmp.astype(dt), vc)

    n_slc = T // SLC_BLOCK
    overlap = jnp.asarray(_overlap_matrix(n_cmp, n_slc))
    imp = jnp.einsum('btgjn,ns->btgs', p_cmp, overlap)
    blk_t = t // SLC_BLOCK
    sblk = jnp.arange(n_slc)
    causal = sblk[None, :] <= blk_t[:, None]
    forced = (sblk[None, :] == 0) | (sblk[None, :] == blk_t[:, None]) | (sblk[None, :] == blk_t[:, None] - 1)
    score = jnp.where(forced[None, :, None, :], FORCE_SCORE,
                      jnp.where(causal[None, :, None, :], imp, -1.0))
    _, sel = lax.top_k(score, min(SLC_TOPN, n_slc))
    sel_valid = sel <= blk_t[None, :, None, None]

    qr = apply_rope(q, cos, sin)
    ks = apply_rope(k_slc, cos, sin)
    kw = apply_rope(k_win, cos, sin)

    ks_blk = ks.reshape(B, n_slc, SLC_BLOCK, NSA_GROUPS, HEAD_DIM).transpose(0, 3, 1, 2, 4)
    vs_blk = v_slc.reshape(B, n_slc, SLC_BLOCK, NSA_GROUPS, HEAD_DIM).transpose(0, 3, 1, 2, 4)
    bidx = jnp.arange(B)[:, None, None, None]
    gidx = jnp.arange(NSA_GROUPS)[None, None, :, None]

    def slc_block(i):
        qb = lax.dynamic_slice_in_dim(qr, i * SLC_Q_BLOCK, SLC_Q_BLOCK, axis=1)
        selb = lax.dynamic_slice_in_dim(sel, i * SLC_Q_BLOCK, SLC_Q_BLOCK, axis=1)
        validb = lax.dynamic_slice_in_dim(sel_valid, i * SLC_Q_BLOCK, SLC_Q_BLOCK, axis=1)
        qpos = i * SLC_Q_BLOCK + jnp.arange(SLC_Q_BLOCK)
        kg = ks_blk[bidx, gidx, selb]
        vg = vs_blk[bidx, gidx, selb]
        kpos = selb[..., None] * SLC_BLOCK + jnp.arange(SLC_BLOCK)
        mask = validb[..., None] & (kpos <= qpos[None, :, None, None, None])
        s = jnp.einsum('bqgjd,bqgnld->bqgjnl', qb, kg).astype(jnp.float32) * scale
        s = jnp.where(mask[:, :, :, None], s, NEG_INF)
        sh = s.shape
        p = jax.nn.softmax(s.reshape(sh[:4] + (-1,)), axis=-1).reshape(sh)
        return jnp.einsum('bqgjnl,bqgnld->bqgjd', p.astype(dt), vg)

    o_slc = _sweep(slc_block, T // SLC_Q_BLOCK)

    kw_pad = jnp.pad(kw, ((0, 0), (WIN, 0), (0, 0), (0, 0)))
    vw_pad = jnp.pad(v_win, ((0, 0), (WIN, 0), (0, 0), (0, 0)))

    def win_block(i):
        qb = lax.dynamic_slice_in_dim(qr, i * Q_BLOCK, Q_BLOCK, axis=1)
        qpos = i * Q_BLOCK + jnp.arange(Q_BLOCK)
        kb = lax.dynamic_slice_in_dim(kw_pad, i * Q_BLOCK, WIN + Q_BLOCK, axis=1)
        vb = lax.dynamic_slice_in_dim(vw_pad, i * Q_BLOCK, WIN + Q_BLOCK, axis=1)
        kpos = i * Q_BLOCK - WIN + jnp.arange(WIN + Q_BLOCK)
        mask = (kpos[None, :] >= 0) & (kpos[None, :] <= qpos[:, None]) & (kpos[None, :] > qpos[:, None] - WIN)
        s = jnp.einsum('bqgjd,bkgd->bqgjk', qb, kb).astype(jnp.float32) * scale
        s = jnp.where(mask[None, :, None, None, :], s, NEG_INF)
        p = jax.nn.softmax(s, axis=-1)
        return jnp.einsum('bqgjk,bkgd->bqgjd', p.astype(dt), vb)

    o_win = _sweep(win_block, T // Q_BLOCK)

    g = jax.nn.sigmoid(gate_logits.astype(jnp.float32)).astype(dt).reshape(B, T, NSA_GROUPS, NSA_HPG, 3)
    o = g[..., 0:1] * o_cmp + g[..., 1:2] * o_slc + g[..., 2:3] * o_win
    return o.reshape(B, T, NSA_Q_W)


def token_mix(h, w_in, da_lambda, da_head_norm, cmp_pos, cmp_w1, cmp_w2, w_proj_da, w_proj_nsa, w_out, lam_init):
    B, T = h.shape[0], h.shape[1]
    proj = h @ w_in
    offs = []
    acc = 0
    for w in IN_WIDTHS[:-1]:
        acc += w
        offs.append(acc)
    (q_da, k_da, v_da, q_ns, kc_raw, vc_raw, ks_raw, vs_raw, kw_raw, vw_raw,
     g_ns, g_a, g_b) = jnp.split(proj, offs, axis=-1)
    cos, sin = rope_tables(T, HEAD_DIM)
    y_da = diff_attention(q_da.reshape(B, T, DA_HEADS, 2, HEAD_DIM),
                          k_da.reshape(B, T, DA_HEADS, 2, HEAD_DIM),
                          v_da.reshape(B, T, DA_HEADS, DA_V_DIM),
                          cos, sin, da_lambda, da_head_norm, lam_init)
    kv = lambda a: a.reshape(B, T, NSA_GROUPS, HEAD_DIM)
    y_ns = nsa(q_ns.reshape(B, T, NSA_GROUPS, NSA_HPG, HEAD_DIM),
               kv(kc_raw), kv(vc_raw), kv(ks_raw), kv(vs_raw), kv(kw_raw), kv(vw_raw),
               g_ns, cos, sin, cmp_pos, cmp_w1, cmp_w2)
    merged = jax.nn.sigmoid(g_a) * (y_da @ w_proj_da) + jax.nn.sigmoid(g_b) * (y_ns @ w_proj_nsa)
    return merged @ w_out


def setup_inputs(seed: int = 0) -> dict:
    key = jax.random.key(seed)
    ks = jax.random.split(key, 24)
    f32 = jnp.float32

    def w(k, shape, fan_in):
        return jax.random.normal(k, shape, f32) * fan_in ** -0.5

    def gain(k, shape):
        return 1.0 + 0.05 * jax.random.normal(k, shape, f32)

    return {
        'x': jax.random.normal(ks[0], (BATCH, SEQ, D_MODEL), f32),
        'ffn1_norm': gain(ks[1], (DEPTH, D_MODEL)),
        'ffn1_w1': w(ks[2], (DEPTH, D_MODEL, D_FF), D_MODEL),
        'ffn1_w3': w(ks[3], (DEPTH, D_MODEL, D_FF), D_MODEL),
        'ffn1_w2': w(ks[4], (DEPTH, D_FF, D_MODEL), D_FF),
        'mix_norm': gain(ks[5], (DEPTH, D_MODEL)),
        'w_in': w(ks[6], (DEPTH, D_MODEL, D_IN), D_MODEL),
        'da_lambda': 0.1 * jax.random.normal(ks[7], (DEPTH, 4, HEAD_DIM), f32),
        'da_head_norm': gain(ks[8], (DEPTH, DA_HEADS, DA_V_DIM)),
        'cmp_pos': 0.1 * jax.random.normal(ks[9], (DEPTH, 2, CMP_BLOCK, HEAD_DIM), f32),
        'cmp_w1': w(ks[10], (DEPTH, 2, CMP_BLOCK * HEAD_DIM, CMP_HIDDEN), CMP_BLOCK * HEAD_DIM),
        'cmp_w2': w(ks[11], (DEPTH, 2, CMP_HIDDEN, HEAD_DIM), CMP_HIDDEN),
        'w_proj_da': w(ks[12], (DEPTH, DA_V_W, D_MODEL), DA_V_W),
        'w_proj_nsa': w(ks[13], (DEPTH, NSA_Q_W, D_MODEL), NSA_Q_W),
        'w_out': w(ks[14], (DEPTH, D_MODEL, D_MODEL), D_MODEL),
        'ffn2_norm': gain(ks[15], (DEPTH, D_MODEL)),
        'ffn2_w1': w(ks[16], (DEPTH, D_MODEL, D_FF), D_MODEL),
        'ffn2_w3': w(ks[17], (DEPTH, D_MODEL, D_FF), D_MODEL),
        'ffn2_w2': w(ks[18], (DEPTH, D_FF, D_MODEL), D_FF),
        'final_norm': gain(ks[19], (D_MODEL,)),
    }


def reference(x, ffn1_norm, ffn1_w1, ffn1_w3, ffn1_w2, mix_norm, w_in, da_lambda, da_head_norm,
              cmp_pos, cmp_w1, cmp_w2, w_proj_da, w_proj_nsa, w_out,
              ffn2_norm, ffn2_w1, ffn2_w3, ffn2_w2, final_norm):
    for l in range(DEPTH):
        lam_init = 0.8 - 0.6 * math.exp(-0.3 * l)
        h = rmsnorm(x, ffn1_norm[l])
        x = x + 0.5 * swiglu(h, ffn1_w1[l], ffn1_w3[l], ffn1_w2[l])
        h = rmsnorm(x, mix_norm[l])
        x = x + token_mix(h, w_in[l], da_lambda[l], da_head_norm[l], cmp_pos[l], cmp_w1[l], cmp_w2[l],
                          w_proj_da[l], w_proj_nsa[l], w_out[l], lam_init)
        h = rmsnorm(x, ffn2_norm[l])
        x = x + 0.5 * swiglu(h, ffn2_w1[l], ffn2_w3[l], ffn2_w2[l])
    return rmsnorm(x, final_norm)
```

```python
import math
from contextlib import ExitStack
import numpy as np
import concourse.bass as bass
import concourse.mybir as mybir
from concourse.bass_utils import run_bass_kernel_spmd

F32 = mybir.dt.float32
BF16 = mybir.dt.bfloat16
AF = mybir.ActivationFunctionType
ALU = mybir.AluOpType
AX = mybir.AxisListType

T = 2048
D = 1024
DFF = 2816
NJ = 22
TS_ = 512
NEG = -30000.0
EPS = 1e-6
LAM_INIT = 0.8 - 0.6 * math.exp(0.0)
N_CORES = 8


class _Instr:
    __slots__ = ("eng", "fn", "deps", "sig", "is_dma", "idx", "waits")


class Emitter:
    ENGS = ("pe", "act", "dve", "pool", "sp")

    def __init__(self, nc):
        self.nc = nc
        self.streams = {e: [] for e in self.ENGS}
        self.state = {}
        self.dma_count = {}
        self.final = []

    def op(self, eng, fn, reads=(), writes=(), dma=None):
        ins = _Instr()
        ins.eng = eng
        ins.fn = fn
        ins.is_dma = dma is not None
        deps = {}

        def add(sig):
            if sig is None:
                return
            k, v = sig
            if deps.get(k, -1) < v:
                deps[k] = v

        for b in reads:
            st = self.state.get(b)
            if st is not None:
                add(st[0])
        for b in writes:
            st = self.state.get(b)
            if st is not None:
                add(st[0])
                for k, v in st[1].items():
                    add((k, v))
        stream = self.streams[eng]
        ins.idx = len(stream)
        if dma is not None:
            c = self.dma_count.get(dma, 0) + 1
            self.dma_count[dma] = c
            ins.sig = (("dma", dma), c)
        else:
            ins.sig = (eng, ins.idx)
        if eng == "pe":
            deps.pop("pe", None)
        ins.deps = deps
        stream.append(ins)
        k, v = ins.sig
        for b in reads:
            st = self.state.get(b)
            if st is None:
                st = [None, {}]
                self.state[b] = st
            if st[1].get(k, -1) < v:
                st[1][k] = v
        for b in writes:
            self.state[b] = [ins.sig, {}]
        return ins

    def wait_all(self, eng, bufs):
        self.final.append((eng, list(bufs)))

    def emit(self):
        nc = self.nc
        needed = {e: set() for e in self.ENGS}
        finals = {e: {} for e in self.ENGS}
        for eng, bufs in self.final:
            for b in bufs:
                st = self.state.get(b)
                if st is not None and st[0] is not None:
                    k, v = st[0]
                    if finals[eng].get(k, -1) < v:
                        finals[eng][k] = v
        for e, stream in self.streams.items():
            waited = {}
            for ins in stream:
                ins.waits = []
                for k, v in ins.deps.items():
                    if waited.get(k, -1) >= v:
                        continue
                    waited[k] = v
                    ins.waits.append((k, v))
                    if not isinstance(k, tuple):
                        needed[k].add(v)
            fw = []
            for k, v in finals[e].items():
                if waited.get(k, -1) >= v:
                    continue
                fw.append((k, v))
                if not isinstance(k, tuple):
                    needed[k].add(v)
            finals[e] = fw
        semval = {}
        for e, stream in self.streams.items():
            arr = [0] * len(stream)
            c = 0
            nd = needed[e]
            for i in range(len(stream)):
                if i in nd:
                    c += 1
                arr[i] = c
            semval[e] = arr
        with ExitStack() as es:
            sems = {}
            for e in self.ENGS:
                sems[e] = es.enter_context(nc.semaphore("s_" + e))
            for slot in self.dma_count:
                sems[("dma", slot)] = es.enter_context(nc.semaphore("d_" + str(slot)))
            block = es.enter_context(nc.Block())

            def run(ename, eng):
                nd = needed[ename]
                for ins in self.streams[ename]:
                    for k, v in ins.waits:
                        if isinstance(k, tuple):
                            eng.wait_ge(sems[k], 16 * v)
                        else:
                            eng.wait_ge(sems[k], semval[k][v])
                    bi = ins.fn(eng)
                    if ins.is_dma:
                        bi.then_inc(sems[ins.sig[0]], 16)
                    elif ins.idx in nd:
                        bi.then_inc(sems[ename], 1)
                for k, v in finals[ename]:
                    if isinstance(k, tuple):
                        eng.wait_ge(sems[k], 16 * v)
                    else:
                        eng.wait_ge(sems[k], semval[k][v])

            @block.tensor
            def _(eng):
                run("pe", eng)

            @block.scalar
            def _(eng):
                run("act", eng)

            @block.vector
            def _(eng):
                run("dve", eng)

            @block.gpsimd
            def _(eng):
                run("pool", eng)

            @block.sync
            def _(eng):
                run("sp", eng)


_PERM = np.array([p + 32 if (p % 64) < 32 else p - 32 for p in range(128)])


def _fm_chunks():
    ch = []
    for h in range(8):
        c = h * 128 + np.arange(128)
        ch += [c, c[_PERM]]
    for h in range(8):
        c = 1024 + h * 128 + np.arange(128)
        ch += [c, c[_PERM]]
    for j in range(8):
        c = np.concatenate([3072 + (0 * 8 + j) * 64 + np.arange(64), 3072 + (8 + j) * 64 + np.arange(64)])
        ch += [c, c[_PERM]]
    ks = 4352 + np.arange(128)
    kw = 4608 + np.arange(128)
    ch += [ks, ks[_PERM], kw, kw[_PERM]]
    raw = [np.concatenate([4096 + g * 64 + np.arange(64), 4224 + g * 64 + np.arange(64)]) for g in range(2)]
    ch += [raw[0], raw[1], raw[0], raw[1]]

    def gate(j, b):
        return np.concatenate([np.full(64, 4864 + (0 * 8 + j) * 3 + b), np.full(64, 4864 + (8 + j) * 3 + b)])

    for j in range(8):
        ch.append(gate(j, 0))
    for j in range(8):
        ch += [gate(j, 1), gate(j, 2)]
    assert len(ch) == 80
    return ch


def _lay_kp(w, cols):
    return np.ascontiguousarray(w[:, cols].reshape(8, 128, len(cols)).transpose(1, 0, 2))


def prep_weights(inp):
    f = np.float32
    o = {}
    for nm in ("ffn1", "ffn2"):
        w1 = np.asarray(inp[nm + "_w1"][0], f)
        w3 = np.asarray(inp[nm + "_w3"][0], f)
        w2 = np.asarray(inp[nm + "_w2"][0], f)
        a = w1.reshape(8, 128, 11, 256).transpose(2, 1, 0, 3)
        b = w3.reshape(8, 128, 11, 256).transpose(2, 1, 0, 3)
        o[nm + "a"] = np.ascontiguousarray(np.stack([a, b], axis=2)).reshape(11, 128, 4096)
        o[nm + "b"] = np.ascontiguousarray(w2.reshape(22, 128, 8, 128).transpose(2, 1, 0, 3)).reshape(8, 128, 2816)
    win = np.asarray(inp["w_in"][0], f)
    ch = _fm_chunks()
    wf = np.stack([_lay_kp(win, c) for c in ch], axis=0)
    o["wf"] = np.ascontiguousarray(wf.reshape(20, 4, 128, 8, 128).transpose(0, 2, 1, 3, 4)).reshape(20, 128, 4096)
    wt = [_lay_kp(win, 2048 + h * 512 + np.arange(512)).reshape(128, 4096) for h in range(2)]
    vsw = _lay_kp(win, np.concatenate([4480 + np.arange(128), 4736 + np.arange(128)])).reshape(128, 2048)
    vsw = np.concatenate([vsw, np.zeros((128, 2048), f)], axis=1)
    o["wt"] = np.ascontiguousarray(np.stack(wt + [vsw], axis=0))
    c1 = np.asarray(inp["cmp_w1"][0], f)
    c1 = c1.reshape(2, 32, 64, 256).transpose(0, 2, 1, 3).reshape(128, 32, 256)
    o["w1c"] = np.ascontiguousarray(c1.reshape(128, 2, 16 * 256).transpose(1, 0, 2))
    wpd = np.asarray(inp["w_proj_da"][0], f)
    wpn = np.asarray(inp["w_proj_nsa"][0], f)
    wo = np.asarray(inp["w_out"][0], f)
    rows_n = np.array([[((p // 64) * 8 + j) * 64 + p % 64 for p in range(128)] for j in range(8)])
    wm = np.zeros((8, 128, 4, 8, 128), f)
    for m in range(8):
        cs = m * 128 + np.arange(128)
        wm[m, :, 0] = _lay_kp(wpd, cs)
        wm[m, :, 1] = wpn[rows_n.T, :][:, :, cs]
        wm[m, :, 2] = _lay_kp(win, 4912 + cs)
        wm[m, :, 3] = _lay_kp(win, 5936 + cs)
    o["wm"] = wm.reshape(8, 128, 4096)
    o["wo"] = np.ascontiguousarray(np.stack([_lay_kp(wo, h * 512 + np.arange(512)).reshape(128, 4096) for h in range(2)], 0))
    g4 = np.stack([np.asarray(inp[k], f).reshape(-1) for k in ("ffn1_norm", "mix_norm", "ffn2_norm", "final_norm")], 0)
    o["g4"] = np.ascontiguousarray(g4.reshape(4, 8, 128).transpose(2, 0, 1))
    o["hg"] = np.ascontiguousarray(np.asarray(inp["da_head_norm"][0], f).T)
    o["lamb"] = np.asarray(inp["da_lambda"][0], f).reshape(1, 256)
    o["pos"] = np.ascontiguousarray(np.asarray(inp["cmp_pos"][0], f).transpose(0, 2, 1).reshape(128, 32))
    w2c = np.asarray(inp["cmp_w2"][0], f)
    w2k = np.zeros((128, 2, 2, 128), f)
    for hc in range(2):
        for g in range(2):
            w2k[:, hc, g, 64 * g:64 * g + 64] = w2c[0, hc * 128:(hc + 1) * 128, :]
    o["w2k"] = w2k.reshape(128, 512)
    o["w2v"] = np.ascontiguousarray(w2c[1].reshape(2, 128, 64).transpose(1, 0, 2)).reshape(128, 128)
    return o


def make_consts():
    f = np.float32
    c = {}
    pos = np.arange(T, dtype=f)
    inv = (1.0 / (10000.0 ** (np.arange(0, 64, 2, dtype=f) / f(64)))).astype(f)
    ang = (pos[:, None] * inv[None, :]).astype(f)
    cos = np.cos(ang).astype(f).T
    sin = np.sin(ang).astype(f).T
    p = np.arange(128)
    sgn = np.where((p % 64) < 32, -1.0, 1.0).astype(f)
    c["cos"] = np.ascontiguousarray(cos[p % 32, :])
    c["sins"] = np.ascontiguousarray(sin[p % 32, :] * sgn[:, None])
    kl = np.arange(128)[:, None]
    ql = np.arange(512)[None, :]
    cm = np.stack([np.where(128 * r + kl <= ql, 0.0, NEG) for r in range(4)], 1)
    wm = np.stack([np.where(128 * r + kl > ql, 0.0, NEG) for r in range(4)], 1)
    c["cmwm"] = np.concatenate([cm, wm], 1).astype(f).reshape(128, 8 * 512)
    n = np.arange(128)[:, None]
    t = np.arange(T)[None, :]
    c["cmask"] = np.where((16 * n + 31 <= t) & (n <= 126), 0.0, NEG).astype(f)
    s = np.arange(32)[:, None]
    xx = np.arange(T)[None, :]
    c["emat"] = (xx // 64 == s).astype(f)
    c["ident"] = np.eye(128, dtype=f)
    ov = np.zeros((128, 64), f)
    for nn in range(127):
        for ss in range(32):
            if nn * 16 < ss * 64 + 64 and nn * 16 + 32 > ss * 64:
                ov[nn, ss] = 1.0
    ov[:, 32:] = 1.0
    c["ov"] = ov
    tt = np.arange(T)
    blk = tt // 64
    sb = np.arange(32)[None, :]
    forced = (sb == 0) | (sb == blk[:, None]) | (sb == blk[:, None] - 1)
    causal = sb <= blk[:, None]
    caus = (causal & ~forced).astype(f)
    addc = np.where(forced, 1e9, np.where(causal, 0.0, -1.0)).astype(f)
    topc = np.stack([caus, addc], 1)
    c["topc"] = np.ascontiguousarray(topc.reshape(16, 128, 2, 32).transpose(1, 0, 2, 3)).reshape(128, 16 * 64)
    return c


def build_program(NSEQ=4, NT=4, stop_after=None, mix_stop=None):
    nc = bass.Bass("TRN2", target_bir_lowering=False)
    em = Emitter(nc)

    def din(name, shape, dt=F32):
        return nc.dram_tensor(name, list(shape), dt, kind="ExternalInput").ap()

    def dscr(name, shape, dt=BF16):
        return nc.dram_tensor(name, list(shape), dt, kind="Internal").ap()

    x_d = din("x", [NSEQ, T, D])
    out_d = nc.dram_tensor("out", [NSEQ, T, D], F32, kind="ExternalOutput").ap()
    wsrc = {}
    wdst = {}
    wshapes = {"ffn1a": [11, 128, 4096], "ffn1b": [8, 128, 2816], "wf": [20, 128, 4096], "wt": [3, 128, 4096],
               "w1c": [2, 128, 4096], "wm": [8, 128, 4096], "wo": [2, 128, 4096],
               "ffn2a": [11, 128, 4096], "ffn2b": [8, 128, 2816]}
    for k, shp in wshapes.items():
        wsrc[k] = din(k, shp)
        wdst[k] = dscr("s_" + k, shp)
    g4_d = din("g4", [128, 32])
    hg_d = din("hg", [128, 8])
    lamb_d = din("lamb", [1, 256])
    pos_d = din("pos", [128, 32])
    w2k_d = din("w2k", [128, 512])
    w2v_d = din("w2v", [128, 128])
    cos_d = din("cos", [128, T])
    sins_d = din("sins", [128, T])
    cmwm_d = din("cmwm", [128, 4096])
    cmask_d = din("cmask", [128, T])
    emat_d = din("emat", [32, T])
    ident_d = din("ident", [128, 128])
    ov_d = din("ov", [128, 64])
    topc_d = din("topc", [128, 1024])
    kda_d = dscr("kda_scr", [8, 128, T])
    vda_d = dscr("vda_scr", [8, 128, 16, 128])

    es = ExitStack()
    with es:
        def sb(name, shape, dt):
            return es.enter_context(nc.sbuf_tensor(name, list(shape), dt))

        PS = [es.enter_context(nc.psum_tensor("ps%d" % i, [128, 512], F32)) for i in range(8)]
        PSN = ["ps%d" % i for i in range(8)]

        XT = sb("XT", [128, 8, 512], F32)
        HT = sb("HT", [128, 8, 512], BF16)
        R1 = sb("R1", [128, 24, 512], BF16)
        R2 = sb("R2", [128, 16, 512], BF16)
        KsT = sb("KsT", [128, T], BF16)
        KwT = sb("KwT", [128, T], BF16)
        VsA = sb("VsA", [128, 16, 2, 128], BF16)
        VwA = sb("VwA", [128, 16, 2, 128], BF16)
        GG = sb("GG", [128, 2, 2, 2, 128], BF16)
        KcT = sb("KcT", [128, 128], BF16)
        VcA = sb("VcA", [128, 2, 128], BF16)
        RAW = sb("RAW", [128, 2, 528], BF16)
        KD = [sb("KD%d" % i, [128, T], BF16) for i in range(2)]
        VD = [sb("VD%d" % i, [128, 16, 128], BF16) for i in range(2)]
        WR = [sb("WR%d" % i, [128, 4096], BF16) for i in range(3)]
        TT_ = [sb("T%d" % i, [128, 512], F32) for i in range(10)]
        PB = [sb("P%d" % i, [128, 512], BF16) for i in range(4)]
        COS = sb("COS", [128, 512], F32)
        SINS = sb("SINS", [128, 512], F32)
        CMWM = sb("CMWM", [128, 8, 512], BF16)
        CMASK = sb("CMASK", [128, 512], BF16)
        EMAT = sb("EMAT", [128, T], BF16)
        IDENT = sb("IDENT", [128, 128], F32)
        IDENTB = sb("IDENTB", [128, 128], BF16)
        ONESB = sb("ONESB", [128, 128], BF16)
        OV = sb("OV", [128, 64], BF16)
        TOPC = sb("TOPC", [128, 4, 2, 32], F32)
        G4 = sb("G4", [128, 4, 8], F32)
        HG = sb("HG", [128, 8], F32)
        LAMB = sb("LAMB", [128, 256], F32)
        LT = sb("LT", [128, 8], F32)
        POS = sb("POS", [128, 32], BF16)
        W2K = sb("W2K", [128, 2, 2, 128], BF16)
        W2V = sb("W2V", [128, 2, 64], BF16)
        CB = sb("CB", [128, 4], F32)
        SELBT = sb("SELBT", [128, 512], BF16)
        IMPACC = [sb("IMPACC%d" % g, [32, 512], F32) for g in range(2)]
        SC = sb("SC", [128, 4, 32], F32)
        SC2 = sb("SC2", [128, 4, 32], F32)
        SELB = sb("SELB", [128, 4, 32], F32)
        M8a = sb("M8a", [128, 8], F32)
        M8b = sb("M8b", [128, 8], F32)
        GEL = [sb("GEL%d" % i, [128, 256], F32) for i in range(3)]
        XIN = sb("XIN", [128, 4, 1024], F32)

        try:
            print("SBUF bytes remaining:", nc.sbuf_bytes_remaining)
        except Exception as ex:
            print("sbuf_bytes_remaining failed", ex)
        def MM(out, lhsT, rhs, start, stop, r, w):
            em.op("pe", lambda e: e.matmul(out, lhsT, rhs, start=start, stop=stop), r, w)

        def TR(out, in_, ident, r, w):
            em.op("pe", lambda e: e.transpose(out, in_, ident), r, w)

        def ACT(out, in_, func, r, w, bias=None, scale=None):
            kw = {}
            if bias is not None:
                kw["bias"] = bias
            if scale is not None:
                kw["scale"] = scale
            em.op("act", lambda e: e.activation(out, in_, func, **kw), r, w)

        def TT(eng, out, in0, in1, op, r, w):
            em.op(eng, lambda e: e.tensor_tensor(out, in0, in1, op), r, w)

        def TSC(eng, out, in0, s1, s2, op0, op1, r, w):
            if op1 is None:
                em.op(eng, lambda e: e.tensor_scalar(out, in0, s1, None, op0), r, w)
            else:
                em.op(eng, lambda e: e.tensor_scalar(out, in0, s1, s2, op0, op1), r, w)

        def STT(eng, out, in0, scalar, in1, op0, op1, r, w):
            em.op(eng, lambda e: e.scalar_tensor_tensor(out, in0, scalar, in1, op0, op1), r, w)

        def CP(eng, out, in_, r, w):
            if eng == "act":
                em.op("act", lambda e: e.activation(out, in_, AF.Copy), r, w)
            else:
                em.op(eng, lambda e: e.tensor_copy(out, in_), r, w)

        def RECIP(out, in_, r, w):
            em.op("dve", lambda e: e.reciprocal(out, in_), r, w)

        def MEMSET(eng, ap, val, w):
            em.op(eng, lambda e: e.memset(ap, val), (), w)

        def DMA(eng, out, in_, r, w, slot):
            em.op(eng, lambda e: e.dma_start(out=out, in_=in_), r, w, dma=slot)

        def r1(u):
            return "R1_%d" % u

        def r2(u):
            return "R2_%d" % u

        cnt = {"p": 0, "s": 0, "t": 0}

        def nextP():
            k = cnt["p"] % 4
            cnt["p"] += 1
            return PB[k], "P%d" % k

        def nextS():
            k = cnt["s"] % 3
            cnt["s"] += 1
            return PS[k], PSN[k]

        for k in ("ffn1a", "ffn1b", "wf", "wt", "w1c", "wm", "wo", "ffn2a", "ffn2b"):
            n = wshapes[k][0]
            for q in range(n):
                last = q == n - 1
                DMA("pool", wdst[k][q], wsrc[k][q], (), ["S_" + k if last else "S_%s_%d" % (k, q)], "cast_" + k)

        def cload(dst_ap, src_ap, name, eng="sp"):
            DMA(eng, dst_ap, src_ap, (), [name], "c_" + name)

        cload(IDENT[:], ident_d, "IDENT")
        cload(G4[:].rearrange("p a b -> p (a b)"), g4_d, "G4")
        cload(HG[:], hg_d, "HG")
        cload(LAMB[:], lamb_d.broadcast_to([128, 256]), "LAMB")
        cload(IDENTB[:], ident_d, "IDENTB", "pool")
        cload(CMWM[:].rearrange("p a b -> p (a b)"), cmwm_d, "CMWM", "pool")
        cload(EMAT[0:32, :], emat_d, "EMAT", "pool")
        cload(EMAT[64:96, :], emat_d, "EMATb", "pool")
        cload(OV[:], ov_d, "OV", "pool")
        cload(POS[:], pos_d, "POS", "pool")
        cload(W2K[:].rearrange("p a b c -> p (a b c)"), w2k_d, "W2K", "pool")
        cload(W2V[:].rearrange("p a b -> p (a b)"), w2v_d, "W2V", "pool")
        MEMSET("pool", ONESB[:], 1.0, ["ONESB"])
        MEMSET("pool", VsA[:, :, :, 64:128], 1.0, ["VsA1"])
        MEMSET("pool", VwA[:, :, :, 64:128], 1.0, ["VwA1"])
        MEMSET("pool", VcA[:, :, 64:128], 1.0, ["VcA1"])
        TT("dve", LAMB[:, 0:64], LAMB[:, 0:64], LAMB[:, 64:128], ALU.mult, ["LAMB"], ["LAMB"])
        TT("dve", LAMB[:, 128:192], LAMB[:, 128:192], LAMB[:, 192:256], ALU.mult, ["LAMB"], ["LAMB"])
        em.op("dve", lambda e: e.reduce_sum(LT[:, 0:1], LAMB[:, 0:64], AX.X), ["LAMB"], ["LT"])
        em.op("dve", lambda e: e.reduce_sum(LT[:, 1:2], LAMB[:, 128:192], AX.X), ["LAMB"], ["LT"])
        ACT(LT[:, 2:4], LT[:, 0:2], AF.Exp, ["LT"], ["LT"])
        TT("dve", LT[:, 4:5], LT[:, 3:4], LT[:, 2:3], ALU.subtract, ["LT"], ["LT"])
        TSC("dve", LT[:, 4:5], LT[:, 4:5], -LAM_INIT, None, ALU.add, None, ["LT"], ["LT"])
        TSC("dve", HG[:], HG[:], 1.0 - LAM_INIT, None, ALU.mult, None, ["HG"], ["HG"])
        NEGLAM = LT[:, 4:5]

        items = []

        import os
        MIXN = int(os.environ.get("MIXN", "100000"))
        mixc = {"n": 0, "on": False}

        def add_w(key, q, nelem, fn):
            if mixc["on"]:
                mixc["n"] += 1
                if mixc["n"] > MIXN:
                    return
            items.append((key, q, nelem, fn))

        def add(fn):
            items.append((None, None, None, fn))

        def cb_item(half):
            def fn(W):
                Wv = W[:, 0:4096].rearrange("p (l h) -> p l h", h=256)
                for which in range(2):
                    pacc, paccn = (PS[7], PSN[7]) if which == 0 else (PS[6], PSN[6])
                    for hc in range(2):
                        col = half * 2 + hc
                        for ll in range(16):
                            l = half * 16 + ll
                            MM(pacc[:, col:col + 1], Wv[64 * which:64 * which + 64, ll, hc * 128:(hc + 1) * 128],
                               POS[64 * which:64 * which + 64, l:l + 1], ll == 0, ll == 15, [W.name_, "POS"], [paccn])
                if half == 1:
                    for which in range(2):
                        pacc, paccn = (PS[7], PSN[7]) if which == 0 else (PS[6], PSN[6])
                        CP("dve", CB[:, which * 2:which * 2 + 2], pacc[:, 0:2], [paccn], ["CB"])
                        TT("dve", CB[:, which * 2:which * 2 + 2], CB[:, which * 2:which * 2 + 2], pacc[:, 2:4], ALU.add, [paccn, "CB"], ["CB"])
            return fn

        add_w("w1c", 0, 4096, cb_item(0))
        add_w("w1c", 1, 4096, cb_item(1))

        def rmsnorm_to_HT(gi):
            for c in range(8):
                P_, pn = nextP()
                ACT(P_[:], XT[:, c, :], AF.Square, ["XT%d" % c], [pn])
                MM(PS[6][:], ONESB[:], P_[:], c == 0, c == 7, [pn, "ONESB"], [PSN[6]])
            ACT(TT_[9][:], PS[6][:], AF.Sqrt, [PSN[6]], ["T9_0", "T9_1"], bias=EPS, scale=1.0 / D)
            RECIP(TT_[9][:], TT_[9][:], ["T9_0", "T9_1"], ["T9_0", "T9_1"])
            for c in range(8):
                STT("dve", HT[:, c, :], XT[:, c, :], G4[:, gi, c:c + 1], TT_[9][:], ALU.mult, ALU.mult,
                    ["XT%d" % c, "G4", "T9_0", "T9_1"], ["HT%d" % c])

        def ffn_items(key_a, key_b, gi):
            def pre():
                rmsnorm_to_HT(gi)
            add(pre)
            for jj in range(11):
                def fa(W, jj=jj):
                    Wv = W[:, 0:4096].rearrange("p (a k c) -> p a k c", a=2, k=8)
                    for jl in range(2):
                        j = 2 * jj + jl
                        pa, pan = PS[j % 2], PSN[j % 2]
                        pb, pbn = PS[2 + j % 2], PSN[2 + j % 2]
                        for k in range(8):
                            MM(pa[:], Wv[:, 0, k, jl * 128:(jl + 1) * 128], HT[:, k, :], k == 0, k == 7,
                               [W.name_, "HT%d" % k], [pan])
                        for k in range(8):
                            MM(pb[:], Wv[:, 1, k, jl * 128:(jl + 1) * 128], HT[:, k, :], k == 0, k == 7,
                               [W.name_, "HT%d" % k], [pbn])
                        tk = j % 2
                        ACT(TT_[tk][:], pa[:], AF.Silu, [pan], ["T%d" % tk])
                        TT("dve", R1[:, j, :], TT_[tk][:], pb[:], ALU.mult, ["T%d" % tk, pbn], [r1(j)])
                add_w(key_a, jj, 4096, fa)
            for m in range(8):
                def fb(W, m=m):
                    Wv = W[:, 0:2816].rearrange("p (j c) -> p j c", c=128)
                    py, pyn = PS[4 + m % 2], PSN[4 + m % 2]
                    for j in range(NJ):
                        MM(py[:], Wv[:, j, :], R1[:, j, :], j == 0, j == NJ - 1, [W.name_, r1(j)], [pyn])
                    STT("dve", XT[:, m, :], py[:], 0.5, XT[:, m, :], ALU.mult, ALU.add, [pyn, "XT%d" % m], ["XT%d" % m])
                add_w(key_b, m, 2816, fb)

        def rope_evac(px, pxn, ps_, psn, out_ap, w):
            ta, tb = cnt["t"] % 2, 2 + cnt["t"] % 2
            cnt["t"] += 1
            TT("dve", TT_[ta][:], px[:], COS[:], ALU.mult, [pxn, "COS"], ["T%d" % ta])
            TT("dve", TT_[tb][:], ps_[:], SINS[:], ALU.mult, [psn, "SINS"], ["T%d" % tb])
            TT("pool", out_ap, TT_[ta][:], TT_[tb][:], ALU.add, ["T%d" % ta, "T%d" % tb], w)

        def mixer_items(s, i):
            mixc["on"] = True
            mixc["n"] = 0
            c0 = i * TS_
            tt0 = 4 * i

            def pre():
                rmsnorm_to_HT(1)
                DMA("sp", COS[:], cos_d[:, c0:c0 + TS_], (), ["COS"], "COS")
                DMA("sp", SINS[:], sins_d[:, c0:c0 + TS_], (), ["SINS"], "SINS")
                DMA("pool", CMASK[:], cmask_d[:, c0:c0 + TS_], (), ["CMASK"], "CMASK")
                DMA("sp", TOPC[:].rearrange("p a b c -> p (a b c)"), topc_d[:, i * 256:(i + 1) * 256], (), ["TOPC"], "TOPC")
                if i == 0:
                    MEMSET("pool", GG[:].rearrange("p a b c d -> p (a b c d)"), 0.0, ["GG"])
            add(pre)

            def proj_pair(Wv, q, k0=8):
                a = (cnt["s"] % 2) * 2
                cnt["s"] += 1
                px, pxn = PS[a], PSN[a]
                ps_, psn = PS[a + 1], PSN[a + 1]
                for k in range(8):
                    MM(px[:], Wv[:, 2 * q, k, :], HT[:, k, :], k == 0, k == 7, [Wv.name_, "HT%d" % k], [pxn])
                for k in range(8):
                    MM(ps_[:], Wv[:, 2 * q + 1, k, :], HT[:, k, :], k == 0, k == 7, [Wv.name_, "HT%d" % k], [psn])
                return px, pxn, ps_, psn

            def wview(W):
                v = W[:, 0:4096].rearrange("p (q k c) -> p q k c", q=4, k=8)
                v.name_ = W.name_
                return v

            for gq in range(4):
                def f(W, gq=gq):
                    Wv = wview(W)
                    for q in range(2):
                        h = 2 * gq + q
                        px, pxn, ps_, psn = proj_pair(Wv, q)
                        rope_evac(px, pxn, ps_, psn, R1[:, h, :], [r1(h)])
                add_w("wf", gq, 4096, f)
            for gq in range(4):
                def f(W, gq=gq):
                    Wv = wview(W)
                    for q in range(2):
                        h = 2 * gq + q
                        px, pxn, ps_, psn = proj_pair(Wv, q)
                        rope_evac(px, pxn, ps_, psn, R2[:, h, :], [r2(h)])
                add_w("wf", 4 + gq, 4096, f)
            for gq in range(4):
                def f(W, gq=gq):
                    Wv = wview(W)
                    for q in range(2):
                        j = 2 * gq + q
                        px, pxn, ps_, psn = proj_pair(Wv, q)
                        if not os.environ.get("NOQN"):
                            CP(os.environ.get("QNENG", "dve"), R1[:, 16 + j, :], px[:], [pxn], [r1(16 + j)])
                        rope_evac(px, pxn, ps_, psn, R1[:, 8 + j, :], [r1(8 + j)])
                add_w("wf", 8 + gq, 4096, f)

            def f_kskw(W):
                Wv = wview(W)
                px, pxn, ps_, psn = proj_pair(Wv, 0)
                rope_evac(px, pxn, ps_, psn, KsT[:, c0:c0 + TS_], ["KsT%d" % i])
                px, pxn, ps_, psn = proj_pair(Wv, 1)
                rope_evac(px, pxn, ps_, psn, KwT[:, c0:c0 + TS_], ["KwT%d" % i])
            add_w("wf", 12, 4096, f_kskw)

            def f_raw(W):
                Wv = wview(W)
                for g in range(2):
                    pp, ppn = PS[4 + g], PSN[4 + g]
                    for k in range(8):
                        MM(pp[:], Wv[:, g, k, :], HT[:, k, :], k == 0, k == 7, [Wv.name_, "HT%d" % k], [ppn])
                    CP("act", RAW[:, g, 16:528], pp[:], [ppn], ["RAW%d" % g])
            add_w("wf", 13, 4096, f_raw)

            for half in range(2):
                def f(W, half=half):
                    Wv = W[:, 0:4096].rearrange("p (k c) -> p k c", k=8)
                    for sub in range(4):
                        pp, ppn = PS[4 + sub % 2], PSN[4 + sub % 2]
                        for k in range(8):
                            MM(pp[:], HT[:, k, sub * 128:(sub + 1) * 128], Wv[:, k, :], k == 0, k == 7,
                               [W.name_, "HT%d" % k], [ppn])
                        u = 8 + 2 * sub + half
                        CP("act" if sub % 2 else "dve", R2[:, u, :], pp[:], [ppn], [r2(u)])
                add_w("wt", half, 4096, f)

            def f_vsw(W):
                Wv = W[:, 0:2048].rearrange("p (k c) -> p k c", k=8)
                for sub in range(4):
                    pp, ppn = PS[4 + sub % 2], PSN[4 + sub % 2]
                    for k in range(8):
                        MM(pp[:, 0:256], HT[:, k, sub * 128:(sub + 1) * 128], Wv[:, k, :], k == 0, k == 7,
                           [W.name_, "HT%d" % k], [ppn])
                    CP("dve", VsA[:, tt0 + sub, :, 0:64], pp[:, 0:128].rearrange("p (g d) -> p g d", g=2), [ppn], ["VsA%d" % (tt0 + sub)])
                    CP("dve", VwA[:, tt0 + sub, :, 0:64], pp[:, 128:256].rearrange("p (g d) -> p g d", g=2), [ppn], ["VwA%d" % (tt0 + sub)])
                import os
                if os.environ.get("NOSPILL"):
                    return
                DMA("pool", kda_d[:, :, c0:c0 + TS_].rearrange("h p c -> p h c"), R2[:, 0:8, :],
                    [r2(u) for u in range(8)], ["KDAD%d" % i], "kst")
                for sub in range(4):
                    src = R2[:, 8 + 2 * sub:10 + 2 * sub, :].rearrange("p a (h e) -> p (a h) e", e=128)
                    DMA("pool", vda_d[:, :, tt0 + sub, :].rearrange("h p e -> p h e"), src,
                        [r2(8 + 2 * sub), r2(9 + 2 * sub)], ["VDAD%d_%d" % (i, sub)], "vst%d" % sub)
            add_w("wt", 2, 2048, f_vsw)

            if mix_stop == "proj":
                return
            nb = 31 if i == 0 else 32
            b0 = 1 if i == 0 else 0
            n0 = 0 if i == 0 else 32 * i - 1
            for half in range(2):
                def f(W, half=half):
                    Wv = W[:, 0:4096].rearrange("p (l h) -> p l h", h=256)
                    for which in range(2):
                        for g in range(2):
                            for hc in range(2):
                                combo = (which * 2 + g) * 2 + hc
                                pacc, paccn = (PS[6], PSN[6]) if which == 0 else (PS[5], PSN[5])
                                cc = half * 4 + g * 2 + hc
                                for ll in range(16):
                                    l = half * 16 + ll
                                    MM(pacc[:, cc * 32:cc * 32 + nb],
                                       Wv[64 * which:64 * which + 64, ll, hc * 128:(hc + 1) * 128],
                                       RAW[64 * which:64 * which + 64, g, l + 16 * b0:l + 16 * b0 + 16 * (nb - 1) + 1:16],
                                       ll == 0, ll == 15, [W.name_, "RAW%d" % g], [paccn])
                    if half == 1:
                        for which in range(2):
                            pacc, paccn = (PS[6], PSN[6]) if which == 0 else (PS[5], PSN[5])
                            for g in range(2):
                                for hc in range(2):
                                    combo = (which * 2 + g) * 2 + hc
                                    cc = g * 2 + hc
                                    col = which * 2 + hc
                                    ACT(GEL[0][:, combo * 32:combo * 32 + 32], pacc[:, cc * 32:cc * 32 + 32], AF.Identity,
                                        [paccn, "CB"], ["GEL0"], bias=CB[:, col:col + 1], scale=1.0)
                            TT("dve", GEL[0][:, which * 128:which * 128 + 128], GEL[0][:, which * 128:which * 128 + 128], pacc[:, 128:256],
                               ALU.add, ["GEL0", paccn], ["GEL0"])
                        TT("dve", GEL[1][:], GEL[0][:], GEL[0][:], ALU.mult, ["GEL0"], ["GEL1"])
                        TSC("dve", GEL[1][:], GEL[1][:], 0.044715, 1.0, ALU.mult, ALU.add, ["GEL1"], ["GEL1"])
                        TT("dve", GEL[1][:], GEL[1][:], GEL[0][:], ALU.mult, ["GEL1", "GEL0"], ["GEL1"])
                        ACT(GEL[2][:], GEL[1][:], AF.Tanh, ["GEL1"], ["GEL2"], scale=0.7978845608028654)
                        TSC("dve", GEL[2][:], GEL[2][:], 0.5, 0.5, ALU.mult, ALU.add, ["GEL2"], ["GEL2"])
                        for which in range(2):
                            for g in range(2):
                                for hc in range(2):
                                    combo = (which * 2 + g) * 2 + hc
                                    TT("dve", GG[:, which, hc, g, n0:n0 + nb], GEL[2][:, combo * 32:combo * 32 + nb],
                                       GEL[0][:, combo * 32:combo * 32 + nb], ALU.mult, ["GEL2", "GEL0"], ["GG"])
                        q = 0
                        for hc in range(2):
                            for g in range(2):
                                MM(PS[7][:, 0:128], W2K[:, hc, g, :], GG[:, 0, hc, g, :], q == 0, q == 3, ["W2K", "GG"], [PSN[7]])
                                q += 1
                        CP("dve", KcT[:], PS[7][:, 0:128], [PSN[7]], ["KcT"])
                        for g in range(2):
                            for hc in range(2):
                                MM(PS[7][:, 128 + 64 * g:192 + 64 * g], GG[:, 1, hc, g, :], W2V[:, hc, :], hc == 0, hc == 1,
                                   ["W2V", "GG"], [PSN[7]])
                        CP("dve", VcA[:, :, 0:64], PS[7][:, 128:256].rearrange("p (g d) -> p g d", g=2), [PSN[7]], ["VcA"])
                        if i < NT - 1:
                            for g in range(2):
                                CP("pool", RAW[:, g, 0:16], RAW[:, g, 512:528], ["RAW%d" % g], ["RAW%d" % g])
                add_w("w1c", half, 4096, f)

            if mix_stop == "compress":
                return

            def gate_tile(Wv, q, tk):
                for k in range(8):
                    MM(PS[7][:], Wv[:, q, k, :], HT[:, k, :], k == 0, k == 7, [Wv.name_, "HT%d" % k], [PSN[7]])
                ACT(TT_[tk][:], PS[7][:], AF.Sigmoid, [PSN[7]], ["T%d" % tk])

            for gq in range(2):
                def f(W, gq=gq):
                    Wv = wview(W)
                    for q in range(4):
                        j = 4 * gq + q
                        gt = 4 + j % 2
                        gate_tile(Wv, q, gt)
                        for g in range(2):
                            lo, hi = 64 * g, 64 * g + 64
                            S_, sn = nextS()
                            MM(S_[:], KcT[lo:hi, :], R1[lo:hi, 16 + j, :], True, False, ["KcT", r1(16 + j)], [sn])
                            MM(S_[:], IDENTB[:], CMASK[:], False, True, ["IDENTB", "CMASK"], [sn])
                            P_, pn = nextP()
                            ACT(P_[:], S_[:], AF.Exp, [sn], [pn], scale=0.125)
                            oc, ocn = PS[3 + g], PSN[3 + g]
                            MM(oc[:], VcA[:, g, :], P_[:], True, True, ["VcA", "VcA1", pn], [ocn])
                            tn = "T6_%d" % g
                            TSC("dve", TT_[6][lo:hi, :], oc[64:128, :], 1e-30, None, ALU.add, None, [ocn], [tn])
                            RECIP(TT_[6][lo:hi, :], TT_[6][lo:hi, :], [tn], [tn])
                            TT("dve", TT_[6][lo:hi, :], TT_[6][lo:hi, :], TT_[gt][lo:hi, :], ALU.mult, [tn, "T%d" % gt], [tn])
                            TT("dve", R2[lo:hi, j, :], oc[0:64, :], TT_[6][lo:hi, :], ALU.mult, [ocn, tn], [r2(j)])
                            if i >= 2:
                                im, imn = PS[5], PSN[5]
                                MM(im[0:64, :], OV[:], P_[:], True, True, ["OV", pn], [imn])
                                TSC("dve", TT_[9][32:64, :], im[32:64, :], 1e-30, None, ALU.add, None, [imn], ["T9_0"])
                                RECIP(TT_[9][32:64, :], TT_[9][32:64, :], ["T9_0"], ["T9_0"])
                                if j == 0:
                                    TT("dve", IMPACC[g][:], im[0:32, :], TT_[9][32:64, :], ALU.mult, [imn, "T9_0"], ["IMPACC%d" % g])
                                else:
                                    TT("dve", TT_[9][0:32, :], im[0:32, :], TT_[9][32:64, :], ALU.mult, [imn, "T9_0"], ["T9_0"])
                                    TT("pool", IMPACC[g][:], IMPACC[g][:], TT_[9][0:32, :], ALU.add, ["IMPACC%d" % g, "T9_0"], ["IMPACC%d" % g])
                add_w("wf", 14 + gq, 4096, f)

            if mix_stop == "cmp":
                return

            def topk():
                if i < 2:
                    return
                for g in range(2):
                    pt, ptn = PS[5], PSN[5]
                    for sub in range(4):
                        TR(pt[:, sub * 32:(sub + 1) * 32], IMPACC[g][0:32, sub * 128:(sub + 1) * 128], IDENT[0:32, 0:32],
                           ["IMPACC%d" % g, "IDENT"], [ptn])
                    TT("dve", SC[:], pt[:, 0:128].rearrange("p (s b) -> p s b", b=32), TOPC[:, :, 0, :], ALU.mult, [ptn, "TOPC"], ["SC"])
                    TT("dve", SC[:], SC[:], TOPC[:, :, 1, :], ALU.add, ["SC", "TOPC"], ["SC"])
                    for sub in range(4):
                        em.op("dve", lambda e, sub=sub: e.max(M8a[:], SC[:, sub, :]), ["SC"], ["M8a"])
                        em.op("dve", lambda e, sub=sub: e.match_replace(SC2[:, sub, :], M8a[:], SC[:, sub, :], -1e30), ["SC", "M8a"], ["SC2"])
                        em.op("dve", lambda e, sub=sub: e.max(M8b[:], SC2[:, sub, :]), ["SC2"], ["M8b"])
                        TSC("dve", SELB[:, sub, :], SC[:, sub, :], M8b[:, 7:8], NEG, ALU.is_lt, ALU.mult, ["SC", "M8b"], ["SELB"])
                    py, pyn = PS[6], PSN[6]
                    for sub in range(4):
                        TR(py[0:32, sub * 128:(sub + 1) * 128], SELB[:, sub, :], IDENT[:], ["SELB", "IDENT"], [pyn])
                    CP("dve", SELBT[64 * g:64 * g + 32, :], py[0:32, :], [pyn], ["SELBT%d" % g])
            add(topk)

            if mix_stop == "topk":
                return

            def score_tile(KT, kname, lo, hi, kt, qap, qname, g, sel, mask_idx):
                S_, sn = nextS()
                last = (not sel) and (mask_idx is None)
                MM(S_[:], KT[lo:hi, kt * 128:(kt + 1) * 128], qap, True, last, [kname, qname], [sn])
                if sel:
                    MM(S_[:], EMAT[64 * g:64 * g + 32, kt * 128:(kt + 1) * 128], SELBT[64 * g:64 * g + 32, :], False, mask_idx is None,
                       ["EMAT", "EMATb", "SELBT%d" % g], [sn])
                if mask_idx is not None:
                    MM(S_[:], IDENTB[:], CMWM[:, mask_idx, :], False, True, ["IDENTB", "CMWM"], [sn])
                P_, pn = nextP()
                ACT(P_[:], S_[:], AF.Exp, [sn], [pn], scale=0.125)
                return P_, pn

            for gq in range(4):
                def f(W, gq=gq):
                    Wv = wview(W)
                    for q in range(2):
                        j = 2 * gq + q
                        gate_tile(Wv, 2 * q, 4)
                        gate_tile(Wv, 2 * q + 1, 5)
                        for g in range(2):
                            lo, hi = 64 * g, 64 * g + 64
                            qap = R1[lo:hi, 8 + j, :]
                            qn = r1(8 + j)
                            os_, osn = PS[3], PSN[3]
                            ow_, own = PS[4], PSN[4]
                            kts = list(range(0, 4 * i + 4))
                            for kt in kts:
                                r = kt - 4 * i
                                P_, pn = score_tile(KsT, "KsT%d" % (kt // 4), lo, hi, kt, qap, qn, g, i >= 2, r if r >= 0 else None)
                                MM(os_[:], VsA[:, kt, g, :], P_[:], kt == kts[0], kt == kts[-1], ["VsA%d" % kt, "VsA1", pn], [osn])
                            ktw = list(range(max(0, 4 * i - 4), 4 * i + 4))
                            for kt in ktw:
                                r = kt - 4 * i
                                midx = r if r >= 0 else 4 + (kt - (4 * i - 4))
                                P_, pn = score_tile(KwT, "KwT%d" % (kt // 4), lo, hi, kt, qap, qn, g, False, midx)
                                MM(ow_[:], VwA[:, kt, g, :], P_[:], kt == ktw[0], kt == ktw[-1], ["VwA%d" % kt, "VwA1", pn], [own])
                            n6, n7, n8, n9 = "T6_%d" % g, "T7_%d" % g, "T8_%d" % g, "T9_%d" % g
                            RECIP(TT_[6][lo:hi, :], os_[64:128, :], [osn], [n6])
                            TT("dve", TT_[6][lo:hi, :], TT_[6][lo:hi, :], TT_[4][lo:hi, :], ALU.mult, [n6, "T4"], [n6])
                            TT("dve", TT_[8][lo:hi, :], os_[0:64, :], TT_[6][lo:hi, :], ALU.mult, [osn, n6], [n8])
                            RECIP(TT_[7][lo:hi, :], ow_[64:128, :], [own], [n7])
                            TT("dve", TT_[7][lo:hi, :], TT_[7][lo:hi, :], TT_[5][lo:hi, :], ALU.mult, [n7, "T5"], [n7])
                            TT("dve", TT_[9][lo:hi, :], ow_[0:64, :], TT_[7][lo:hi, :], ALU.mult, [own, n7], [n9])
                            TT("pool", TT_[8][lo:hi, :], TT_[8][lo:hi, :], TT_[9][lo:hi, :], ALU.add, [n8, n9], [n8])
                            TT("pool", R2[lo:hi, 8 + j, :], TT_[8][lo:hi, :], R2[lo:hi, j, :], ALU.add, [n8, r2(j)], [r2(8 + j)])
                add_w("wf", 16 + gq, 4096, f)

            if mix_stop == "slc":
                return

            def da_load(h):
                sl = h % 2
                ncol = TS_ * (i + 1)
                DMA("sp", KD[sl][:, 0:ncol], kda_d[h, :, 0:ncol], ["KDAD%d" % q for q in range(i + 1)], ["KD%d" % sl], "KD%d" % sl)
                DMA("sp", VD[sl][:, 0:4 * (i + 1), :], vda_d[h, :, 0:4 * (i + 1), :],
                    ["VDAD%d_%d" % (q, sub) for q in range(i + 1) for sub in range(4)], ["VD%d" % sl], "VD%d" % sl)

            def da_all():
                da_load(0)
                for h in range(8):
                    if h + 1 < 8:
                        da_load(h + 1)
                    sl = h % 2
                    kts = list(range(0, 4 * i + 4))
                    for c in range(2):
                        lo, hi = 64 * c, 64 * c + 64
                        od, odn = PS[3 + 2 * c], PSN[3 + 2 * c]
                        sd, sdn = PS[4 + 2 * c], PSN[4 + 2 * c]
                        for kt in kts:
                            r = kt - 4 * i
                            P_, pn = score_tile(KD[sl], "KD%d" % sl, lo, hi, kt, R1[lo:hi, h, :], r1(h), 0, False, r if r >= 0 else None)
                            MM(od[:], VD[sl][:, kt, :], P_[:], kt == 0, kt == kts[-1], ["VD%d" % sl, pn], [odn])
                            MM(sd[:], ONESB[:], P_[:], kt == 0, kt == kts[-1], ["ONESB", pn], [sdn])
                    RECIP(TT_[0][:], PS[4][:], [PSN[4]], ["T0"])
                    TT("dve", TT_[0][:], PS[3][:], TT_[0][:], ALU.mult, [PSN[3], "T0"], ["T0"])
                    RECIP(TT_[1][:], PS[6][:], [PSN[6]], ["T1"])
                    TT("dve", TT_[1][:], PS[5][:], TT_[1][:], ALU.mult, [PSN[5], "T1"], ["T1"])
                    STT("dve", TT_[2][:], TT_[1][:], NEGLAM, TT_[0][:], ALU.mult, ALU.add, ["T1", "T0", "LT"], ["T2"])
                    P_, pn = nextP()
                    ACT(P_[:], TT_[2][:], AF.Square, ["T2"], [pn])
                    MM(PS[7][:], ONESB[:], P_[:], True, True, ["ONESB", pn], [PSN[7]])
                    ACT(TT_[3][:], PS[7][:], AF.Sqrt, [PSN[7]], ["T3"], bias=EPS, scale=1.0 / 128.0)
                    RECIP(TT_[3][:], TT_[3][:], ["T3"], ["T3"])
                    STT("dve", R1[:, 16 + h, :], TT_[2][:], HG[:, h:h + 1], TT_[3][:], ALU.mult, ALU.mult, ["T2", "HG", "T3"], [r1(16 + h)])
            add(da_all)

            if mix_stop == "da":
                return

            for m in range(8):
                def f(W, m=m):
                    Wv = wview(W)
                    for k in range(8):
                        MM(PS[0][:], Wv[:, 0, k, :], R1[:, 16 + k, :], k == 0, k == 7, [Wv.name_, r1(16 + k)], [PSN[0]])
                    for k in range(8):
                        MM(PS[1][:], Wv[:, 1, k, :], R2[:, 8 + k, :], k == 0, k == 7, [Wv.name_, r2(8 + k)], [PSN[1]])
                    for k in range(8):
                        MM(PS[2][:], Wv[:, 2, k, :], HT[:, k, :], k == 0, k == 7, [Wv.name_, "HT%d" % k], [PSN[2]])
                    for k in range(8):
                        MM(PS[3][:], Wv[:, 3, k, :], HT[:, k, :], k == 0, k == 7, [Wv.name_, "HT%d" % k], [PSN[3]])
                    ACT(TT_[0][:], PS[2][:], AF.Sigmoid, [PSN[2]], ["T0"])
                    ACT(TT_[1][:], PS[3][:], AF.Sigmoid, [PSN[3]], ["T1"])
                    TT("dve", TT_[0][:], PS[0][:], TT_[0][:], ALU.mult, [PSN[0], "T0"], ["T0"])
                    TT("dve", TT_[1][:], PS[1][:], TT_[1][:], ALU.mult, [PSN[1], "T1"], ["T1"])
                    TT("pool", R1[:, 8 + m, :], TT_[0][:], TT_[1][:], ALU.add, ["T0", "T1"], [r1(8 + m)])
                add_w("wm", m, 4096, f)
            for half in range(2):
                def f(W, half=half):
                    Wv = W[:, 0:4096].rearrange("p (k c) -> p k c", k=8)
                    for ml in range(4):
                        m = 4 * half + ml
                        py, pyn = PS[4 + ml % 2], PSN[4 + ml % 2]
                        for k in range(8):
                            MM(py[:], Wv[:, k, ml * 128:(ml + 1) * 128], R1[:, 8 + k, :], k == 0, k == 7, [W.name_, r1(8 + k)], [pyn])
                        TT("dve", XT[:, m, :], py[:], XT[:, m, :], ALU.add, [pyn, "XT%d" % m], ["XT%d" % m])
                add_w("wo", half, 4096, f)

        def load_x(s, i):
            def fn():
                c0 = i * TS_
                DMA("sp", XIN[:], x_d[s, c0:c0 + TS_, :].rearrange("(a p) f -> p a f", p=128), (), ["XIN"], "XIN")
                for c in range(8):
                    pp, ppn = PS[c % 2], PSN[c % 2]
                    for sub in range(4):
                        TR(pp[:, sub * 128:(sub + 1) * 128], XIN[:, sub, c * 128:(c + 1) * 128], IDENT[:], ["XIN", "IDENT"], [ppn])
                    CP("act" if c % 2 else "dve", XT[:, c, :], pp[:], [ppn], ["XT%d" % c])
            add(fn)

        def store_out(s, i):
            def fn():
                c0 = i * TS_
                for c in range(8):
                    P_, pn = nextP()
                    ACT(P_[:], XT[:, c, :], AF.Square, ["XT%d" % c], [pn])
                    MM(PS[6][:], ONESB[:], P_[:], c == 0, c == 7, [pn, "ONESB"], [PSN[6]])
                ACT(TT_[9][:], PS[6][:], AF.Sqrt, [PSN[6]], ["T9_0", "T9_1"], bias=EPS, scale=1.0 / D)
                RECIP(TT_[9][:], TT_[9][:], ["T9_0", "T9_1"], ["T9_0", "T9_1"])
                for c in range(8):
                    STT("dve", XT[:, c, :], XT[:, c, :], G4[:, 3, c:c + 1], TT_[9][:], ALU.mult, ALU.mult,
                        ["XT%d" % c, "G4", "T9_0", "T9_1"], ["XT%d" % c])
                for sub in range(4):
                    for cc in range(2):
                        pp, ppn = PS[cc], PSN[cc]
                        for cl in range(4):
                            c = 4 * cc + cl
                            TR(pp[:, cl * 128:(cl + 1) * 128], XT[:, c, sub * 128:(sub + 1) * 128], IDENT[:], ["XT%d" % c, "IDENT"], [ppn])
                        CP("act" if cc else "dve", XIN[:, sub, cc * 512:(cc + 1) * 512], pp[:], [ppn], ["XIN"])
                DMA("sp", out_d[s, c0:c0 + TS_, :].rearrange("(a p) f -> p a f", p=128), XIN[:], ["XIN"], ["OUT"], "OUT")
            add(fn)

        for s in range(NSEQ):
            for i in range(NT):
                load_x(s, i)
                if stop_after != "load":
                    ffn_items("ffn1a", "ffn1b", 0)
                    if stop_after != "ffn1":
                        mixer_items(s, i)
                        mixc["on"] = False
                        if stop_after != "mix":
                            ffn_items("ffn2a", "ffn2b", 2)
                store_out(s, i)

        widx = [k for k, it in enumerate(items) if it[0] is not None]
        state = {"next": 0}

        def issue_upto(n):
            while state["next"] <= n and state["next"] < len(widx):
                q = state["next"]
                key, ci, nelem, _ = items[widx[q]]
                slot = q % 3
                DMA("sp", WR[slot][:, 0:nelem], wdst[key][ci][:, 0:nelem], ["S_" + key], ["WR%d" % slot], "WR%d" % slot)
                state["next"] += 1

        wq = 0
        for k, it in enumerate(items):
            key, ci, nelem, fn = it
            if key is None:
                fn()
            else:
                issue_upto(wq + 2)
                slot = wq % 3
                fn(_WProxy(WR[slot], "WR%d" % slot))
                wq += 1

        em.wait_all("sp", ["OUT"])
        em.emit()
    return nc


class _WProxy:
    def __init__(self, t, name):
        self.t = t
        self.name_ = name

    def __getitem__(self, key):
        return _APProxy(self.t[key], self.name_)


class _APProxy:
    def __init__(self, ap, name):
        self.ap = ap
        self.name_ = name

    def rearrange(self, *a, **k):
        return _APProxy(self.ap.rearrange(*a, **k), self.name_)

    def __getitem__(self, key):
        return self.ap[key]


_CACHE = {}


def kernel(**inputs):
    x = np.asarray(inputs["x"], np.float32)
    B = x.shape[0]
    nseq = B // N_CORES
    w = prep_weights(inputs)
    c = make_consts()
    key = ("prog", nseq)
    if key not in _CACHE:
        _CACHE[key] = build_program(NSEQ=nseq, NT=4)
    nc = _CACHE[key]
    shared = {}
    shared.update(w)
    shared.update(c)
    in_maps = []
    for core in range(N_CORES):
        m = dict(shared)
        m["x"] = np.ascontiguousarray(x[core * nseq:(core + 1) * nseq])
        in_maps.append(m)
    res = run_bass_kernel_spmd(nc, in_maps, core_ids=list(range(N_CORES)))
    out = np.concatenate([r["out"] for r in res.results], axis=0)
    return out.astype(np.float32)
```

```python
import math
from contextlib import ExitStack
import numpy as np
import concourse.bass as bass
import concourse.mybir as mybir
from concourse.bass_utils import run_bass_kernel_spmd

F32 = mybir.dt.float32
BF16 = mybir.dt.bfloat16
AF = mybir.ActivationFunctionType
ALU = mybir.AluOpType
AX = mybir.AxisListType

T = 2048
D = 1024
DFF = 2816
NJ = 22
TS_ = 512
NEG = -30000.0
EPS = 1e-6
LAM_INIT = 0.8 - 0.6 * math.exp(0.0)
N_CORES = 8


class _Instr:
    __slots__ = ("eng", "fn", "deps", "sig", "is_dma", "idx", "waits")


class Emitter:
    ENGS = ("pe", "act", "dve", "pool", "sp")

    def __init__(self, nc):
        self.nc = nc
        self.streams = {e: [] for e in self.ENGS}
        self.state = {}
        self.dma_count = {}
        self.final = []

    def op(self, eng, fn, reads=(), writes=(), dma=None):
        ins = _Instr()
        ins.eng = eng
        ins.fn = fn
        ins.is_dma = dma is not None
        deps = {}

        def add(sig):
            if sig is None:
                return
            k, v = sig
            if deps.get(k, -1) < v:
                deps[k] = v

        for b in reads:
            st = self.state.get(b)
            if st is not None:
                add(st[0])
        for b in writes:
            st = self.state.get(b)
            if st is not None:
                add(st[0])
                for k, v in st[1].items():
                    add((k, v))
        stream = self.streams[eng]
        ins.idx = len(stream)
        if dma is not None:
            c = self.dma_count.get(dma, 0) + 1
            self.dma_count[dma] = c
            ins.sig = (("dma", dma), c)
        else:
            ins.sig = (eng, ins.idx)
        if eng == "pe":
            deps.pop("pe", None)
        ins.deps = deps
        stream.append(ins)
        k, v = ins.sig
        for b in reads:
            st = self.state.get(b)
            if st is None:
                st = [None, {}]
                self.state[b] = st
            if st[1].get(k, -1) < v:
                st[1][k] = v
        for b in writes:
            self.state[b] = [ins.sig, {}]
        return ins

    def wait_all(self, eng, bufs):
        self.final.append((eng, list(bufs)))

    def emit(self):
        nc = self.nc
        needed = {e: set() for e in self.ENGS}
        finals = {e: {} for e in self.ENGS}
        for eng, bufs in self.final:
            for b in bufs:
                st = self.state.get(b)
                if st is not None and st[0] is not None:
                    k, v = st[0]
                    if finals[eng].get(k, -1) < v:
                        finals[eng][k] = v
        for e, stream in self.streams.items():
            waited = {}
            for ins in stream:
                ins.waits = []
                for k, v in ins.deps.items():
                    if waited.get(k, -1) >= v:
                        continue
                    waited[k] = v
                    ins.waits.append((k, v))
                    if not isinstance(k, tuple):
                        needed[k].add(v)
            fw = []
            for k, v in finals[e].items():
                if waited.get(k, -1) >= v:
                    continue
                fw.append((k, v))
                if not isinstance(k, tuple):
                    needed[k].add(v)
            finals[e] = fw
        semval = {}
        for e, stream in self.streams.items():
            arr = [0] * len(stream)
            c = 0
            nd = needed[e]
            for i in range(len(stream)):
                if i in nd:
                    c += 1
                arr[i] = c
            semval[e] = arr
        with ExitStack() as es:
            sems = {}
            for e in self.ENGS:
                sems[e] = es.enter_context(nc.semaphore("s_" + e))
            for slot in self.dma_count:
                sems[("dma", slot)] = es.enter_context(nc.semaphore("d_" + str(slot)))
            block = es.enter_context(nc.Block())

            def run(ename, eng):
                nd = needed[ename]
                for ins in self.streams[ename]:
                    for k, v in ins.waits:
                        if isinstance(k, tuple):
                            eng.wait_ge(sems[k], 16 * v)
                        else:
                            eng.wait_ge(sems[k], semval[k][v])
                    bi = ins.fn(eng)
                    if ins.is_dma:
                        bi.then_inc(sems[ins.sig[0]], 16)
                    elif ins.idx in nd:
                        bi.then_inc(sems[ename], 1)
                for k, v in finals[ename]:
                    if isinstance(k, tuple):
                        eng.wait_ge(sems[k], 16 * v)
                    else:
                        eng.wait_ge(sems[k], semval[k][v])

            @block.tensor
            def _(eng):
                run("pe", eng)

            @block.scalar
            def _(eng):
                run("act", eng)

            @block.vector
            def _(eng):
                run("dve", eng)

            @block.gpsimd
            def _(eng):
                run("pool", eng)

            @block.sync
            def _(eng):
                run("sp", eng)


_PERM = np.array([p + 32 if (p % 64) < 32 else p - 32 for p in range(128)])


def _fm_chunks():
    ch = []
    for h in range(8):
        c = h * 128 + np.arange(128)
        ch += [c, c[_PERM]]
    for h in range(8):
        c = 1024 + h * 128 + np.arange(128)
        ch += [c, c[_PERM]]
    for j in range(8):
        c = np.concatenate([3072 + (0 * 8 + j) * 64 + np.arange(64), 3072 + (8 + j) * 64 + np.arange(64)])
        ch += [c, c[_PERM]]
    ks = 4352 + np.arange(128)
    kw = 4608 + np.arange(128)
    ch += [ks, ks[_PERM], kw, kw[_PERM]]
    raw = [np.concatenate([4096 + g * 64 + np.arange(64), 4224 + g * 64 + np.arange(64)]) for g in range(2)]
    ch += [raw[0], raw[1], raw[0], raw[1]]

    def gate(j, b):
        return np.concatenate([np.full(64, 4864 + (0 * 8 + j) * 3 + b), np.full(64, 4864 + (8 + j) * 3 + b)])

    for j in range(8):
        ch.append(gate(j, 0))
    for j in range(8):
        ch += [gate(j, 1), gate(j, 2)]
    assert len(ch) == 80
    return ch


def _lay_kp(w, cols):
    return np.ascontiguousarray(w[:, cols].reshape(8, 128, len(cols)).transpose(1, 0, 2))


def prep_weights(inp):
    f = np.float32
    o = {}
    for nm in ("ffn1", "ffn2"):
        w1 = np.asarray(inp[nm + "_w1"][0], f)
        w3 = np.asarray(inp[nm + "_w3"][0], f)
        w2 = np.asarray(inp[nm + "_w2"][0], f)
        a = w1.reshape(8, 128, 11, 256).transpose(2, 1, 0, 3)
        b = w3.reshape(8, 128, 11, 256).transpose(2, 1, 0, 3)
        o[nm + "a"] = np.ascontiguousarray(np.stack([a, b], axis=2)).reshape(11, 128, 4096)
        o[nm + "b"] = np.ascontiguousarray(w2.reshape(22, 128, 8, 128).transpose(2, 1, 0, 3)).reshape(8, 128, 2816)
    win = np.asarray(inp["w_in"][0], f)
    ch = _fm_chunks()
    wf = np.stack([_lay_kp(win, c) for c in ch], axis=0)
    o["wf"] = np.ascontiguousarray(wf.reshape(20, 4, 128, 8, 128).transpose(0, 2, 1, 3, 4)).reshape(20, 128, 4096)
    wt = [_lay_kp(win, 2048 + h * 512 + np.arange(512)).reshape(128, 4096) for h in range(2)]
    vsw = _lay_kp(win, np.concatenate([4480 + np.arange(128), 4736 + np.arange(128)])).reshape(128, 2048)
    vsw = np.concatenate([vsw, np.zeros((128, 2048), f)], axis=1)
    o["wt"] = np.ascontiguousarray(np.stack(wt + [vsw], axis=0))
    c1 = np.asarray(inp["cmp_w1"][0], f)
    c1 = c1.reshape(2, 32, 64, 256).transpose(0, 2, 1, 3).reshape(128, 32, 256)
    o["w1c"] = np.ascontiguousarray(c1.reshape(128, 2, 16 * 256).transpose(1, 0, 2))
    wpd = np.asarray(inp["w_proj_da"][0], f)
    wpn = np.asarray(inp["w_proj_nsa"][0], f)
    wo = np.asarray(inp["w_out"][0], f)
    rows_n = np.array([[((p // 64) * 8 + j) * 64 + p % 64 for p in range(128)] for j in range(8)])
    wm = np.zeros((8, 128, 4, 8, 128), f)
    for m in range(8):
        cs = m * 128 + np.arange(128)
        wm[m, :, 0] = _lay_kp(wpd, cs)
        wm[m, :, 1] = wpn[rows_n.T, :][:, :, cs]
        wm[m, :, 2] = _lay_kp(win, 4912 + cs)
        wm[m, :, 3] = _lay_kp(win, 5936 + cs)
    o["wm"] = wm.reshape(8, 128, 4096)
    o["wo"] = np.ascontiguousarray(np.stack([_lay_kp(wo, h * 512 + np.arange(512)).reshape(128, 4096) for h in range(2)], 0))
    g4 = np.stack([np.asarray(inp[k], f).reshape(-1) for k in ("ffn1_norm", "mix_norm", "ffn2_norm", "final_norm")], 0)
    o["g4"] = np.ascontiguousarray(g4.reshape(4, 8, 128).transpose(2, 0, 1)).reshape(128, 32)
    o["hg"] = np.ascontiguousarray(np.asarray(inp["da_head_norm"][0], f).T)
    o["lamb"] = np.asarray(inp["da_lambda"][0], f).reshape(1, 256)
    o["pos"] = np.ascontiguousarray(np.asarray(inp["cmp_pos"][0], f).transpose(0, 2, 1).reshape(128, 32))
    w2c = np.asarray(inp["cmp_w2"][0], f)
    w2k = np.zeros((128, 2, 2, 128), f)
    for hc in range(2):
        for g in range(2):
            w2k[:, hc, g, 64 * g:64 * g + 64] = w2c[0, hc * 128:(hc + 1) * 128, :]
    o["w2k"] = w2k.reshape(128, 512)
    o["w2v"] = np.ascontiguousarray(w2c[1].reshape(2, 128, 64).transpose(1, 0, 2)).reshape(128, 128)
    return o


def make_consts():
    f = np.float32
    c = {}
    pos = np.arange(T, dtype=f)
    inv = (1.0 / (10000.0 ** (np.arange(0, 64, 2, dtype=f) / f(64)))).astype(f)
    ang = (pos[:, None] * inv[None, :]).astype(f)
    cos = np.cos(ang).astype(f).T
    sin = np.sin(ang).astype(f).T
    p = np.arange(128)
    sgn = np.where((p % 64) < 32, -1.0, 1.0).astype(f)
    c["cos"] = np.ascontiguousarray(cos[p % 32, :])
    c["sins"] = np.ascontiguousarray(sin[p % 32, :] * sgn[:, None])
    kl = np.arange(128)[:, None]
    ql = np.arange(512)[None, :]
    cm = np.stack([np.where(128 * r + kl <= ql, 0.0, NEG) for r in range(4)], 1)
    wm = np.stack([np.where(128 * r + kl > ql, 0.0, NEG) for r in range(4)], 1)
    c["cmwm"] = np.concatenate([cm, wm], 1).astype(f).reshape(128, 8 * 512)
    n = np.arange(128)[:, None]
    t = np.arange(T)[None, :]
    c["cmask"] = np.where((16 * n + 31 <= t) & (n <= 126), 0.0, NEG).astype(f)
    s = np.arange(32)[:, None]
    xx = np.arange(T)[None, :]
    c["emat"] = (xx // 64 == s).astype(f)
    c["ident"] = np.eye(128, dtype=f)
    ov = np.zeros((128, 64), f)
    for nn in range(127):
        for ss in range(32):
            if nn * 16 < ss * 64 + 64 and nn * 16 + 32 > ss * 64:
                ov[nn, ss] = 1.0
    ov[:, 32:] = 1.0
    c["ov"] = ov
    tt = np.arange(T)
    blk = tt // 64
    sb = np.arange(32)[None, :]
    forced = (sb == 0) | (sb == blk[:, None]) | (sb == blk[:, None] - 1)
    causal = sb <= blk[:, None]
    caus = (causal & ~forced).astype(f)
    addc = np.where(forced, 1e9, np.where(causal, 0.0, -1.0)).astype(f)
    topc = np.stack([caus, addc], 1)
    c["topc"] = np.ascontiguousarray(topc.reshape(16, 128, 2, 32).transpose(1, 0, 2, 3)).reshape(128, 16 * 64)
    return c


def build_program(NSEQ=4, NT=4, stop_after=None, mix_stop=None):
    nc = bass.Bass("TRN2", target_bir_lowering=False)
    em = Emitter(nc)

    def din(name, shape, dt=F32):
        return nc.dram_tensor(name, list(shape), dt, kind="ExternalInput").ap()

    def dscr(name, shape, dt=BF16):
        return nc.dram_tensor(name, list(shape), dt, kind="Internal").ap()

    x_d = din("x", [NSEQ, T, D])
    out_d = nc.dram_tensor("out", [NSEQ, T, D], F32, kind="ExternalOutput").ap()
    wsrc = {}
    wdst = {}
    wshapes = {"ffn1a": [11, 128, 4096], "ffn1b": [8, 128, 2816], "wf": [20, 128, 4096], "wt": [3, 128, 4096],
               "w1c": [2, 128, 4096], "wm": [8, 128, 4096], "wo": [2, 128, 4096],
               "ffn2a": [11, 128, 4096], "ffn2b": [8, 128, 2816]}
    for k, shp in wshapes.items():
        wsrc[k] = din(k, shp)
        wdst[k] = dscr("s_" + k, shp)
    g4_d = din("g4", [128, 32])
    hg_d = din("hg", [128, 8])
    lamb_d = din("lamb", [1, 256])
    pos_d = din("pos", [128, 32])
    w2k_d = din("w2k", [128, 512])
    w2v_d = din("w2v", [128, 128])
    cos_d = din("cos", [128, T])
    sins_d = din("sins", [128, T])
    cmwm_d = din("cmwm", [128, 4096])
    cmask_d = din("cmask", [128, T])
    emat_d = din("emat", [32, T])
    ident_d = din("ident", [128, 128])
    ov_d = din("ov", [128, 64])
    topc_d = din("topc", [128, 1024])
    kda_d = dscr("kda_scr", [8, 128, T])
    vda_d = dscr("vda_scr", [8, 128, 16, 128])

    es = ExitStack()
    with es:
        def sb(name, shape, dt):
            return es.enter_context(nc.sbuf_tensor(name, list(shape), dt))

        PS = [es.enter_context(nc.psum_tensor("ps%d" % i, [128, 512], F32)) for i in range(8)]
        PSN = ["ps%d" % i for i in range(8)]

        XT = sb("XT", [128, 8, 512], F32)
        HT = sb("HT", [128, 8, 512], BF16)
        R1 = sb("R1", [128, 24, 512], BF16)
        R2 = sb("R2", [128, 16, 512], BF16)
        KsT = sb("KsT", [128, T], BF16)
        KwT = sb("KwT", [128, T], BF16)
        VsA = sb("VsA", [128, 16, 2, 128], BF16)
        VwA = sb("VwA", [128, 16, 2, 128], BF16)
        GG = sb("GG", [128, 2, 2, 2, 128], BF16)
        KcT = sb("KcT", [128, 128], BF16)
        VcA = sb("VcA", [128, 2, 128], BF16)
        RAW = sb("RAW", [128, 2, 528], BF16)
        KD = [sb("KD%d" % i, [128, T], BF16) for i in range(2)]
        VD = [sb("VD%d" % i, [128, 16, 128], BF16) for i in range(2)]
        WR = [sb("WR%d" % i, [128, 4096], BF16) for i in range(3)]
        TT_ = [sb("T%d" % i, [128, 512], F32) for i in range(10)]
        PB = [sb("P%d" % i, [128, 512], BF16) for i in range(4)]
        COS = sb("COS", [128, 512], F32)
        SINS = sb("SINS", [128, 512], F32)
        CMWM = sb("CMWM", [128, 8, 512], BF16)
        CMASK = sb("CMASK", [128, 512], BF16)
        EMAT = sb("EMAT", [128, T], BF16)
        IDENT = sb("IDENT", [128, 128], F32)
        IDENTB = sb("IDENTB", [128, 128], BF16)
        ONESB = sb("ONESB", [128, 128], BF16)
        OV = sb("OV", [128, 64], BF16)
        TOPC = sb("TOPC", [128, 4, 2, 32], F32)
        G4 = sb("G4", [128, 4, 8], F32)
        HG = sb("HG", [128, 8], F32)
        LAMB = sb("LAMB", [128, 256], F32)
        LT = sb("LT", [128, 8], F32)
        POS = sb("POS", [128, 32], BF16)
        W2K = sb("W2K", [128, 2, 2, 128], BF16)
        W2V = sb("W2V", [128, 2, 64], BF16)
        CB = sb("CB", [128, 4], F32)
        SELBT = sb("SELBT", [128, 512], BF16)
        IMPACC = [sb("IMPACC%d" % g, [32, 512], F32) for g in range(2)]
        SC = sb("SC", [128, 4, 32], F32)
        SC2 = sb("SC2", [128, 4, 32], F32)
        SELB = sb("SELB", [128, 4, 32], F32)
        M8a = sb("M8a", [128, 8], F32)
        M8b = sb("M8b", [128, 8], F32)
        GEL = [sb("GEL%d" % i, [128, 256], F32) for i in range(3)]
        XIN = sb("XIN", [128, 4, 1024], F32)

        try:
            print("SBUF bytes remaining:", nc.sbuf_bytes_remaining)
        except Exception as ex:
            print("sbuf_bytes_remaining failed", ex)
        def MM(out, lhsT, rhs, start, stop, r, w):
            em.op("pe", lambda e: e.matmul(out, lhsT, rhs, start=start, stop=stop), r, w)

        def TR(out, in_, ident, r, w):
            em.op("pe", lambda e: e.transpose(out, in_, ident), r, w)

        def ACT(out, in_, func, r, w, bias=None, scale=None):
            kw = {}
            if bias is not None:
                kw["bias"] = bias
            if scale is not None:
                kw["scale"] = scale
            em.op("act", lambda e: e.activation(out, in_, func, **kw), r, w)

        def TT(eng, out, in0, in1, op, r, w):
            em.op(eng, lambda e: e.tensor_tensor(out, in0, in1, op), r, w)

        def TSC(eng, out, in0, s1, s2, op0, op1, r, w):
            if op1 is None:
                em.op(eng, lambda e: e.tensor_scalar(out, in0, s1, None, op0), r, w)
            else:
                em.op(eng, lambda e: e.tensor_scalar(out, in0, s1, s2, op0, op1), r, w)

        def STT(eng, out, in0, scalar, in1, op0, op1, r, w):
            em.op(eng, lambda e: e.scalar_tensor_tensor(out, in0, scalar, in1, op0, op1), r, w)

        def CP(eng, out, in_, r, w):
            if eng == "act":
                em.op("act", lambda e: e.activation(out, in_, AF.Copy), r, w)
            else:
                em.op(eng, lambda e: e.tensor_copy(out, in_), r, w)

        def RECIP(out, in_, r, w):
            em.op("dve", lambda e: e.reciprocal(out, in_), r, w)

        def MEMSET(eng, ap, val, w):
            em.op(eng, lambda e: e.memset(ap, val), (), w)

        def DMA(eng, out, in_, r, w, slot):
            em.op(eng, lambda e: e.dma_start(out=out, in_=in_), r, w, dma=slot)

        def r1(u):
            return "R1_%d" % u

        def r2(u):
            return "R2_%d" % u

        cnt = {"p": 0, "s": 0, "t": 0}

        def nextP():
            k = cnt["p"] % 4
            cnt["p"] += 1
            return PB[k], "P%d" % k

        def nextS():
            k = cnt["s"] % 3
            cnt["s"] += 1
            return PS[k], PSN[k]

        for k in ("ffn1a", "ffn1b", "wf", "wt", "w1c", "wm", "wo", "ffn2a", "ffn2b"):
            n = wshapes[k][0]
            for q in range(n):
                last = q == n - 1
                DMA("pool", wdst[k][q], wsrc[k][q], (), ["S_" + k if last else "S_%s_%d" % (k, q)], "cast_" + k)

        def cload(dst_ap, src_ap, name, eng="sp"):
            DMA(eng, dst_ap, src_ap, (), [name], "c_" + name)

        cload(IDENT[:], ident_d, "IDENT")
        cload(G4[:].rearrange("p a b -> p (a b)"), g4_d, "G4")
        cload(HG[:], hg_d, "HG")
        cload(LAMB[:], lamb_d.broadcast_to([128, 256]), "LAMB")
        cload(IDENTB[:], ident_d, "IDENTB", "pool")
        cload(CMWM[:].rearrange("p a b -> p (a b)"), cmwm_d, "CMWM", "pool")
        cload(EMAT[0:32, :], emat_d, "EMAT", "pool")
        cload(EMAT[64:96, :], emat_d, "EMATb", "pool")
        cload(OV[:], ov_d, "OV", "pool")
        cload(POS[:], pos_d, "POS", "pool")
        cload(W2K[:].rearrange("p a b c -> p (a b c)"), w2k_d, "W2K", "pool")
        cload(W2V[:].rearrange("p a b -> p (a b)"), w2v_d, "W2V", "pool")
        MEMSET("pool", ONESB[:], 1.0, ["ONESB"])
        MEMSET("pool", VsA[:, :, :, 64:128], 1.0, ["VsA1"])
        MEMSET("pool", VwA[:, :, :, 64:128], 1.0, ["VwA1"])
        MEMSET("pool", VcA[:, :, 64:128], 1.0, ["VcA1"])
        TT("dve", LAMB[:, 0:64], LAMB[:, 0:64], LAMB[:, 64:128], ALU.mult, ["LAMB"], ["LAMB"])
        TT("dve", LAMB[:, 128:192], LAMB[:, 128:192], LAMB[:, 192:256], ALU.mult, ["LAMB"], ["LAMB"])
        em.op("dve", lambda e: e.reduce_sum(LT[:, 0:1], LAMB[:, 0:64], AX.X), ["LAMB"], ["LT"])
        em.op("dve", lambda e: e.reduce_sum(LT[:, 1:2], LAMB[:, 128:192], AX.X), ["LAMB"], ["LT"])
        ACT(LT[:, 2:4], LT[:, 0:2], AF.Exp, ["LT"], ["LT"])
        TT("dve", LT[:, 4:5], LT[:, 3:4], LT[:, 2:3], ALU.subtract, ["LT"], ["LT"])
        TSC("dve", LT[:, 4:5], LT[:, 4:5], -LAM_INIT, None, ALU.add, None, ["LT"], ["LT"])
        TSC("dve", HG[:], HG[:], 1.0 - LAM_INIT, None, ALU.mult, None, ["HG"], ["HG"])
        NEGLAM = LT[:, 4:5]

        items = []

        import os
        MIXN = int(os.environ.get("MIXN", "100000"))
        mixc = {"n": 0, "on": False}

        def add_w(key, q, nelem, fn):
            if mixc["on"]:
                mixc["n"] += 1
                if mixc["n"] > MIXN:
                    return
            items.append((key, q, nelem, fn))

        def add(fn):
            items.append((None, None, None, fn))

        def cb_item(half):
            def fn(W):
                Wv = W[:, 0:4096].rearrange("p (l h) -> p l h", h=256)
                for which in range(2):
                    pacc, paccn = (PS[7], PSN[7]) if which == 0 else (PS[6], PSN[6])
                    for hc in range(2):
                        col = half * 2 + hc
                        for ll in range(16):
                            l = half * 16 + ll
                            MM(pacc[:, col:col + 1], Wv[64 * which:64 * which + 64, ll, hc * 128:(hc + 1) * 128],
                               POS[64 * which:64 * which + 64, l:l + 1], ll == 0, ll == 15, [W.name_, "POS"], [paccn])
                if half == 1:
                    for which in range(2):
                        pacc, paccn = (PS[7], PSN[7]) if which == 0 else (PS[6], PSN[6])
                        CP("dve", CB[:, which * 2:which * 2 + 2], pacc[:, 0:2], [paccn], ["CB"])
                        TT("dve", CB[:, which * 2:which * 2 + 2], CB[:, which * 2:which * 2 + 2], pacc[:, 2:4], ALU.add, [paccn, "CB"], ["CB"])
            return fn

        add_w("w1c", 0, 4096, cb_item(0))
        add_w("w1c", 1, 4096, cb_item(1))

        def rmsnorm_to_HT(gi):
            for c in range(8):
                P_, pn = nextP()
                ACT(P_[:], XT[:, c, :], AF.Square, ["XT%d" % c], [pn])
                MM(PS[6][:], ONESB[:], P_[:], c == 0, c == 7, [pn, "ONESB"], [PSN[6]])
            ACT(TT_[9][:], PS[6][:], AF.Sqrt, [PSN[6]], ["T9_0", "T9_1"], bias=EPS, scale=1.0 / D)
            RECIP(TT_[9][:], TT_[9][:], ["T9_0", "T9_1"], ["T9_0", "T9_1"])
            for c in range(8):
                STT("dve", HT[:, c, :], XT[:, c, :], G4[:, gi, c:c + 1], TT_[9][:], ALU.mult, ALU.mult,
                    ["XT%d" % c, "G4", "T9_0", "T9_1"], ["HT%d" % c])

        def ffn_items(key_a, key_b, gi):
            def pre():
                rmsnorm_to_HT(gi)
            add(pre)
            for jj in range(11):
                def fa(W, jj=jj):
                    Wv = W[:, 0:4096].rearrange("p (a k c) -> p a k c", a=2, k=8)
                    for jl in range(2):
                        j = 2 * jj + jl
                        pa, pan = PS[j % 2], PSN[j % 2]
                        pb, pbn = PS[2 + j % 2], PSN[2 + j % 2]
                        for k in range(8):
                            MM(pa[:], Wv[:, 0, k, jl * 128:(jl + 1) * 128], HT[:, k, :], k == 0, k == 7,
                               [W.name_, "HT%d" % k], [pan])
                        for k in range(8):
                            MM(pb[:], Wv[:, 1, k, jl * 128:(jl + 1) * 128], HT[:, k, :], k == 0, k == 7,
                               [W.name_, "HT%d" % k], [pbn])
                        tk = j % 2
                        ACT(TT_[tk][:], pa[:], AF.Silu, [pan], ["T%d" % tk])
                        TT("dve", R1[:, j, :], TT_[tk][:], pb[:], ALU.mult, ["T%d" % tk, pbn], [r1(j)])
                add_w(key_a, jj, 4096, fa)
            for m in range(8):
                def fb(W, m=m):
                    Wv = W[:, 0:2816].rearrange("p (j c) -> p j c", c=128)
                    py, pyn = PS[4 + m % 2], PSN[4 + m % 2]
                    for j in range(NJ):
                        MM(py[:], Wv[:, j, :], R1[:, j, :], j == 0, j == NJ - 1, [W.name_, r1(j)], [pyn])
                    STT("dve", XT[:, m, :], py[:], 0.5, XT[:, m, :], ALU.mult, ALU.add, [pyn, "XT%d" % m], ["XT%d" % m])
                add_w(key_b, m, 2816, fb)

        def rope_evac(px, pxn, ps_, psn, out_ap, w):
            ta, tb = cnt["t"] % 2, 2 + cnt["t"] % 2
            cnt["t"] += 1
            TT("dve", TT_[ta][:], px[:], COS[:], ALU.mult, [pxn, "COS"], ["T%d" % ta])
            TT("dve", TT_[tb][:], ps_[:], SINS[:], ALU.mult, [psn, "SINS"], ["T%d" % tb])
            TT("pool", out_ap, TT_[ta][:], TT_[tb][:], ALU.add, ["T%d" % ta, "T%d" % tb], w)

        def mixer_items(s, i):
            mixc["on"] = True
            mixc["n"] = 0
            c0 = i * TS_
            tt0 = 4 * i

            def pre():
                rmsnorm_to_HT(1)
                DMA("sp", COS[:], cos_d[:, c0:c0 + TS_], (), ["COS"], "COS")
                DMA("sp", SINS[:], sins_d[:, c0:c0 + TS_], (), ["SINS"], "SINS")
                DMA("pool", CMASK[:], cmask_d[:, c0:c0 + TS_], (), ["CMASK"], "CMASK")
                DMA("sp", TOPC[:].rearrange("p a b c -> p (a b c)"), topc_d[:, i * 256:(i + 1) * 256], (), ["TOPC"], "TOPC")
                if i == 0:
                    MEMSET("pool", GG[:].rearrange("p a b c d -> p (a b c d)"), 0.0, ["GG"])
            add(pre)

            def proj_pair(Wv, q, k0=8):
                a = (cnt["s"] % 2) * 2
                cnt["s"] += 1
                px, pxn = PS[a], PSN[a]
                ps_, psn = PS[a + 1], PSN[a + 1]
                for k in range(8):
                    MM(px[:], Wv[:, 2 * q, k, :], HT[:, k, :], k == 0, k == 7, [Wv.name_, "HT%d" % k], [pxn])
                for k in range(8):
                    MM(ps_[:], Wv[:, 2 * q + 1, k, :], HT[:, k, :], k == 0, k == 7, [Wv.name_, "HT%d" % k], [psn])
                return px, pxn, ps_, psn

            def wview(W):
                v = W[:, 0:4096].rearrange("p (q k c) -> p q k c", q=4, k=8)
                v.name_ = W.name_
                return v

            for gq in range(4):
                def f(W, gq=gq):
                    Wv = wview(W)
                    for q in range(2):
                        h = 2 * gq + q
                        px, pxn, ps_, psn = proj_pair(Wv, q)
                        rope_evac(px, pxn, ps_, psn, R1[:, h, :], [r1(h)])
                add_w("wf", gq, 4096, f)
            for gq in range(4):
                def f(W, gq=gq):
                    Wv = wview(W)
                    for q in range(2):
                        h = 2 * gq + q
                        px, pxn, ps_, psn = proj_pair(Wv, q)
                        rope_evac(px, pxn, ps_, psn, R2[:, h, :], [r2(h)])
                add_w("wf", 4 + gq, 4096, f)
            for gq in range(4):
                def f(W, gq=gq):
                    Wv = wview(W)
                    for q in range(2):
                        j = 2 * gq + q
                        px, pxn, ps_, psn = proj_pair(Wv, q)
                        if not os.environ.get("NOQN"):
                            CP(os.environ.get("QNENG", "dve"), R1[:, 16 + j, :], px[:], [pxn], [r1(16 + j)])
                        rope_evac(px, pxn, ps_, psn, R1[:, 8 + j, :], [r1(8 + j)])
                add_w("wf", 8 + gq, 4096, f)

            def f_kskw(W):
                Wv = wview(W)
                px, pxn, ps_, psn = proj_pair(Wv, 0)
                rope_evac(px, pxn, ps_, psn, KsT[:, c0:c0 + TS_], ["KsT%d" % i])
                px, pxn, ps_, psn = proj_pair(Wv, 1)
                rope_evac(px, pxn, ps_, psn, KwT[:, c0:c0 + TS_], ["KwT%d" % i])
            add_w("wf", 12, 4096, f_kskw)

            def f_raw(W):
                Wv = wview(W)
                for g in range(2):
                    pp, ppn = PS[4 + g], PSN[4 + g]
                    for k in range(8):
                        MM(pp[:], Wv[:, g, k, :], HT[:, k, :], k == 0, k == 7, [Wv.name_, "HT%d" % k], [ppn])
                    CP("act", RAW[:, g, 16:528], pp[:], [ppn], ["RAW%d" % g])
            add_w("wf", 13, 4096, f_raw)

            for half in range(2):
                def f(W, half=half):
                    Wv = W[:, 0:4096].rearrange("p (k c) -> p k c", k=8)
                    for sub in range(4):
                        pp, ppn = PS[4 + sub % 2], PSN[4 + sub % 2]
                        for k in range(8):
                            MM(pp[:], HT[:, k, sub * 128:(sub + 1) * 128], Wv[:, k, :], k == 0, k == 7,
                               [W.name_, "HT%d" % k], [ppn])
                        u = 8 + 2 * sub + half
                        CP("act" if sub % 2 else "dve", R2[:, u, :], pp[:], [ppn], [r2(u)])
                add_w("wt", half, 4096, f)

            def f_vsw(W):
                Wv = W[:, 0:2048].rearrange("p (k c) -> p k c", k=8)
                for sub in range(4):
                    pp, ppn = PS[4 + sub % 2], PSN[4 + sub % 2]
                    for k in range(8):
                        MM(pp[:, 0:256], HT[:, k, sub * 128:(sub + 1) * 128], Wv[:, k, :], k == 0, k == 7,
                           [W.name_, "HT%d" % k], [ppn])
                    CP("dve", VsA[:, tt0 + sub, :, 0:64], pp[:, 0:128].rearrange("p (g d) -> p g d", g=2), [ppn], ["VsA%d" % (tt0 + sub)])
                    CP("dve", VwA[:, tt0 + sub, :, 0:64], pp[:, 128:256].rearrange("p (g d) -> p g d", g=2), [ppn], ["VwA%d" % (tt0 + sub)])
                import os
                if os.environ.get("NOSPILL"):
                    return
                DMA("pool", kda_d[:, :, c0:c0 + TS_].rearrange("h p c -> p h c"), R2[:, 0:8, :],
                    [r2(u) for u in range(8)], ["KDAD%d" % i], "kst")
                for sub in range(4):
                    src = R2[:, 8 + 2 * sub:10 + 2 * sub, :].rearrange("p a (h e) -> p (a h) e", e=128)
                    DMA("pool", vda_d[:, :, tt0 + sub, :].rearrange("h p e -> p h e"), src,
                        [r2(8 + 2 * sub), r2(9 + 2 * sub)], ["VDAD%d_%d" % (i, sub)], "vst%d" % sub)
            add_w("wt", 2, 2048, f_vsw)

            if mix_stop == "proj":
                return
            nb = 31 if i == 0 else 32
            b0 = 1 if i == 0 else 0
            n0 = 0 if i == 0 else 32 * i - 1
            for half in range(2):
                def f(W, half=half):
                    Wv = W[:, 0:4096].rearrange("p (l h) -> p l h", h=256)
                    for which in range(2):
                        for g in range(2):
                            for hc in range(2):
                                combo = (which * 2 + g) * 2 + hc
                                pacc, paccn = (PS[6], PSN[6]) if which == 0 else (PS[5], PSN[5])
                                cc = half * 4 + g * 2 + hc
                                for ll in range(16):
                                    l = half * 16 + ll
                                    MM(pacc[:, cc * 32:cc * 32 + nb],
                                       Wv[64 * which:64 * which + 64, ll, hc * 128:(hc + 1) * 128],
                                       RAW[64 * which:64 * which + 64, g, l + 16 * b0:l + 16 * b0 + 16 * (nb - 1) + 1:16],
                                       ll == 0, ll == 15, [W.name_, "RAW%d" % g], [paccn])
                    if half == 1:
                        for which in range(2):
                            pacc, paccn = (PS[6], PSN[6]) if which == 0 else (PS[5], PSN[5])
                            for g in range(2):
                                for hc in range(2):
                                    combo = (which * 2 + g) * 2 + hc
                                    cc = g * 2 + hc
                                    col = which * 2 + hc
                                    ACT(GEL[0][:, combo * 32:combo * 32 + 32], pacc[:, cc * 32:cc * 32 + 32], AF.Identity,
                                        [paccn, "CB"], ["GEL0"], bias=CB[:, col:col + 1], scale=1.0)
                            TT("dve", GEL[0][:, which * 128:which * 128 + 128], GEL[0][:, which * 128:which * 128 + 128], pacc[:, 128:256],
                               ALU.add, ["GEL0", paccn], ["GEL0"])
                        TT("dve", GEL[1][:], GEL[0][:], GEL[0][:], ALU.mult, ["GEL0"], ["GEL1"])
                        TSC("dve", GEL[1][:], GEL[1][:], 0.044715, 1.0, ALU.mult, ALU.add, ["GEL1"], ["GEL1"])
                        TT("dve", GEL[1][:], GEL[1][:], GEL[0][:], ALU.mult, ["GEL1", "GEL0"], ["GEL1"])
                        ACT(GEL[2][:], GEL[1][:], AF.Tanh, ["GEL1"], ["GEL2"], scale=0.7978845608028654)
                        TSC("dve", GEL[2][:], GEL[2][:], 0.5, 0.5, ALU.mult, ALU.add, ["GEL2"], ["GEL2"])
                        for which in range(2):
                            for g in range(2):
                                for hc in range(2):
                                    combo = (which * 2 + g) * 2 + hc
                                    TT("dve", GG[:, which, hc, g, n0:n0 + nb], GEL[2][:, combo * 32:combo * 32 + nb],
                                       GEL[0][:, combo * 32:combo * 32 + nb], ALU.mult, ["GEL2", "GEL0"], ["GG"])
                        q = 0
                        for hc in range(2):
                            for g in range(2):
                                MM(PS[7][:, 0:128], W2K[:, hc, g, :], GG[:, 0, hc, g, :], q == 0, q == 3, ["W2K", "GG"], [PSN[7]])
                                q += 1
                        CP("dve", KcT[:], PS[7][:, 0:128], [PSN[7]], ["KcT"])
                        for g in range(2):
                            for hc in range(2):
                                MM(PS[7][:, 128 + 64 * g:192 + 64 * g], GG[:, 1, hc, g, :], W2V[:, hc, :], hc == 0, hc == 1,
                                   ["W2V", "GG"], [PSN[7]])
                        CP("dve", VcA[:, :, 0:64], PS[7][:, 128:256].rearrange("p (g d) -> p g d", g=2), [PSN[7]], ["VcA"])
                        if i < NT - 1:
                            for g in range(2):
                                CP("pool", RAW[:, g, 0:16], RAW[:, g, 512:528], ["RAW%d" % g], ["RAW%d" % g])
                add_w("w1c", half, 4096, f)

            if mix_stop == "compress":
                return

            def gate_tile(Wv, q, tk):
                for k in range(8):
                    MM(PS[7][:], Wv[:, q, k, :], HT[:, k, :], k == 0, k == 7, [Wv.name_, "HT%d" % k], [PSN[7]])
                ACT(TT_[tk][:], PS[7][:], AF.Sigmoid, [PSN[7]], ["T%d" % tk])

            for gq in range(2):
                def f(W, gq=gq):
                    Wv = wview(W)
                    for q in range(4):
                        j = 4 * gq + q
                        gt = 4 + j % 2
                        gate_tile(Wv, q, gt)
                        for g in range(2):
                            lo, hi = 64 * g, 64 * g + 64
                            S_, sn = nextS()
                            MM(S_[:], KcT[lo:hi, :], R1[lo:hi, 16 + j, :], True, False, ["KcT", r1(16 + j)], [sn])
                            MM(S_[:], IDENTB[:], CMASK[:], False, True, ["IDENTB", "CMASK"], [sn])
                            P_, pn = nextP()
                            ACT(P_[:], S_[:], AF.Exp, [sn], [pn], scale=0.125)
                            oc, ocn = PS[3 + g], PSN[3 + g]
                            MM(oc[:], VcA[:, g, :], P_[:], True, True, ["VcA", "VcA1", pn], [ocn])
                            tn = "T6_%d" % g
                            TSC("dve", TT_[6][lo:hi, :], oc[64:128, :], 1e-30, None, ALU.add, None, [ocn], [tn])
                            RECIP(TT_[6][lo:hi, :], TT_[6][lo:hi, :], [tn], [tn])
                            TT("dve", TT_[6][lo:hi, :], TT_[6][lo:hi, :], TT_[gt][lo:hi, :], ALU.mult, [tn, "T%d" % gt], [tn])
                            TT("dve", R2[lo:hi, j, :], oc[0:64, :], TT_[6][lo:hi, :], ALU.mult, [ocn, tn], [r2(j)])
                            if i >= 2:
                                im, imn = PS[5], PSN[5]
                                MM(im[0:64, :], OV[:], P_[:], True, True, ["OV", pn], [imn])
                                TSC("dve", TT_[9][32:64, :], im[32:64, :], 1e-30, None, ALU.add, None, [imn], ["T9_0"])
                                RECIP(TT_[9][32:64, :], TT_[9][32:64, :], ["T9_0"], ["T9_0"])
                                if j == 0:
                                    TT("dve", IMPACC[g][:], im[0:32, :], TT_[9][32:64, :], ALU.mult, [imn, "T9_0"], ["IMPACC%d" % g])
                                else:
                                    TT("dve", TT_[9][0:32, :], im[0:32, :], TT_[9][32:64, :], ALU.mult, [imn, "T9_0"], ["T9_0"])
                                    TT("pool", IMPACC[g][:], IMPACC[g][:], TT_[9][0:32, :], ALU.add, ["IMPACC%d" % g, "T9_0"], ["IMPACC%d" % g])
                add_w("wf", 14 + gq, 4096, f)

            if mix_stop == "cmp":
                return

            def topk():
                if i < 2:
                    return
                for g in range(2):
                    pt, ptn = PS[5], PSN[5]
                    for sub in range(4):
                        TR(pt[:, sub * 32:(sub + 1) * 32], IMPACC[g][0:32, sub * 128:(sub + 1) * 128], IDENT[0:32, 0:32],
                           ["IMPACC%d" % g, "IDENT"], [ptn])
                    TT("dve", SC[:], pt[:, 0:128].rearrange("p (s b) -> p s b", b=32), TOPC[:, :, 0, :], ALU.mult, [ptn, "TOPC"], ["SC"])
                    TT("dve", SC[:], SC[:], TOPC[:, :, 1, :], ALU.add, ["SC", "TOPC"], ["SC"])
                    for sub in range(4):
                        em.op("dve", lambda e, sub=sub: e.max(M8a[:], SC[:, sub, :]), ["SC"], ["M8a"])
                        em.op("dve", lambda e, sub=sub: e.match_replace(SC2[:, sub, :], M8a[:], SC[:, sub, :], -1e30), ["SC", "M8a"], ["SC2"])
                        em.op("dve", lambda e, sub=sub: e.max(M8b[:], SC2[:, sub, :]), ["SC2"], ["M8b"])
                        TSC("dve", SELB[:, sub, :], SC[:, sub, :], M8b[:, 7:8], NEG, ALU.is_lt, ALU.mult, ["SC", "M8b"], ["SELB"])
                    py, pyn = PS[6], PSN[6]
                    for sub in range(4):
                        TR(py[0:32, sub * 128:(sub + 1) * 128], SELB[:, sub, :], IDENT[:], ["SELB", "IDENT"], [pyn])
                    CP("dve", SELBT[64 * g:64 * g + 32, :], py[0:32, :], [pyn], ["SELBT%d" % g])
            add(topk)

            if mix_stop == "topk":
                return

            def score_tile(KT, kname, lo, hi, kt, qap, qname, g, sel, mask_idx):
                S_, sn = nextS()
                last = (not sel) and (mask_idx is None)
                MM(S_[:], KT[lo:hi, kt * 128:(kt + 1) * 128], qap, True, last, [kname, qname], [sn])
                if sel:
                    MM(S_[:], EMAT[64 * g:64 * g + 32, kt * 128:(kt + 1) * 128], SELBT[64 * g:64 * g + 32, :], False, mask_idx is None,
                       ["EMAT", "EMATb", "SELBT%d" % g], [sn])
                if mask_idx is not None:
                    MM(S_[:], IDENTB[:], CMWM[:, mask_idx, :], False, True, ["IDENTB", "CMWM"], [sn])
                P_, pn = nextP()
                ACT(P_[:], S_[:], AF.Exp, [sn], [pn], scale=0.125)
                return P_, pn

            def run_pipeline(steps, LA=2):
                pend = []
                for score_fn, post_fn in steps:
                    Pp = score_fn()
                    pend.append((post_fn, Pp))
                    if len(pend) > LA:
                        f_, a_ = pend.pop(0)
                        f_(*a_)
                for f_, a_ in pend:
                    f_(*a_)

            for gq in range(4):
                def f(W, gq=gq):
                    Wv = wview(W)
                    steps = []
                    for q in range(2):
                        j = 2 * gq + q
                        tgs, tgw = (4, 5) if j % 2 == 0 else (0, 1)
                        for g in range(2):
                            lo, hi = 64 * g, 64 * g + 64
                            qap = R1[lo:hi, 8 + j, :]
                            qn = r1(8 + j)
                            par = (j * 2 + g) % 2
                            os_, osn = PS[3 + 2 * par], PSN[3 + 2 * par]
                            ow_, own = PS[4 + 2 * par], PSN[4 + 2 * par]
                            kts = list(range(0, 4 * i + 4))
                            ktw = list(range(max(0, 4 * i - 4), 4 * i + 4))
                            first = True
                            for kt in kts:
                                r = kt - 4 * i

                                def sc(kt=kt, r=r, lo=lo, hi=hi, qap=qap, qn=qn, g=g, first=first, q=q, tgs=tgs, tgw=tgw):
                                    if first and g == 0:
                                        gate_tile(Wv, 2 * q, tgs)
                                        gate_tile(Wv, 2 * q + 1, tgw)
                                    return score_tile(KsT, "KsT%d" % (kt // 4), lo, hi, kt, qap, qn, g, i >= 2, r if r >= 0 else None)

                                def po(P_, pn, kt=kt, g=g, os_=os_, osn=osn, kts=kts):
                                    MM(os_[:], VsA[:, kt, g, :], P_[:], kt == kts[0], kt == kts[-1], ["VsA%d" % kt, "VsA1", pn], [osn])
                                steps.append((sc, po))
                                first = False
                            for kt in ktw:
                                r = kt - 4 * i
                                midx = r if r >= 0 else 4 + (kt - (4 * i - 4))

                                def sc(kt=kt, midx=midx, lo=lo, hi=hi, qap=qap, qn=qn, g=g):
                                    return score_tile(KwT, "KwT%d" % (kt // 4), lo, hi, kt, qap, qn, g, False, midx)

                                def po(P_, pn, kt=kt, g=g, j=j, lo=lo, hi=hi, os_=os_, osn=osn, ow_=ow_, own=own, ktw=ktw, tgs=tgs, tgw=tgw):
                                    MM(ow_[:], VwA[:, kt, g, :], P_[:], kt == ktw[0], kt == ktw[-1], ["VwA%d" % kt, "VwA1", pn], [own])
                                    if kt != ktw[-1]:
                                        return
                                    n6, n7, n8, n9 = "T6_%d" % g, "T7_%d" % g, "T8_%d" % g, "T9_%d" % g
                                    RECIP(TT_[6][lo:hi, :], os_[64:128, :], [osn], [n6])
                                    TT("dve", TT_[6][lo:hi, :], TT_[6][lo:hi, :], TT_[tgs][lo:hi, :], ALU.mult, [n6, "T%d" % tgs], [n6])
                                    TT("dve", TT_[8][lo:hi, :], os_[0:64, :], TT_[6][lo:hi, :], ALU.mult, [osn, n6], [n8])
                                    RECIP(TT_[7][lo:hi, :], ow_[64:128, :], [own], [n7])
                                    TT("dve", TT_[7][lo:hi, :], TT_[7][lo:hi, :], TT_[tgw][lo:hi, :], ALU.mult, [n7, "T%d" % tgw], [n7])
                                    TT("dve", TT_[9][lo:hi, :], ow_[0:64, :], TT_[7][lo:hi, :], ALU.mult, [own, n7], [n9])
                                    TT("pool", TT_[8][lo:hi, :], TT_[8][lo:hi, :], TT_[9][lo:hi, :], ALU.add, [n8, n9], [n8])
                                    TT("pool", R2[lo:hi, 8 + j, :], TT_[8][lo:hi, :], R2[lo:hi, j, :], ALU.add, [n8, r2(j)], [r2(8 + j)])
                                steps.append((sc, po))
                    run_pipeline(steps)
                add_w("wf", 16 + gq, 4096, f)

            if mix_stop == "slc":
                return

            def da_load(h):
                sl = h % 2
                ncol = TS_ * (i + 1)
                DMA("sp", KD[sl][:, 0:ncol], kda_d[h, :, 0:ncol], ["KDAD%d" % q for q in range(i + 1)], ["KD%d" % sl], "KD%d" % sl)
                DMA("sp", VD[sl][:, 0:4 * (i + 1), :], vda_d[h, :, 0:4 * (i + 1), :],
                    ["VDAD%d_%d" % (q, sub) for q in range(i + 1) for sub in range(4)], ["VD%d" % sl], "VD%d" % sl)

            def da_all():
                da_load(0)
                steps = []
                for h in range(8):
                    sl = h % 2
                    kts = list(range(0, 4 * i + 4))
                    for c in range(2):
                        lo, hi = 64 * c, 64 * c + 64
                        od, odn = PS[3 + 2 * c], PSN[3 + 2 * c]
                        sd, sdn = PS[4 + 2 * c], PSN[4 + 2 * c]
                        for kt in kts:
                            r = kt - 4 * i

                            def sc(kt=kt, r=r, lo=lo, hi=hi, h=h, sl=sl):
                                return score_tile(KD[sl], "KD%d" % sl, lo, hi, kt, R1[lo:hi, h, :], r1(h), 0, False, r if r >= 0 else None)

                            def po(P_, pn, kt=kt, c=c, h=h, sl=sl, od=od, odn=odn, sd=sd, sdn=sdn, kts=kts):
                                if c == 0 and kt == 0 and h + 1 < 8:
                                    da_load(h + 1)
                                MM(od[:], VD[sl][:, kt, :], P_[:], kt == 0, kt == kts[-1], ["VD%d" % sl, pn], [odn])
                                MM(sd[:], ONESB[:], P_[:], kt == 0, kt == kts[-1], ["ONESB", pn], [sdn])
                                if not (c == 1 and kt == kts[-1]):
                                    return
                                RECIP(TT_[0][:], PS[4][:], [PSN[4]], ["T0"])
                                TT("dve", TT_[0][:], PS[3][:], TT_[0][:], ALU.mult, [PSN[3], "T0"], ["T0"])
                                RECIP(TT_[1][:], PS[6][:], [PSN[6]], ["T1"])
                                TT("dve", TT_[1][:], PS[5][:], TT_[1][:], ALU.mult, [PSN[5], "T1"], ["T1"])
                                STT("dve", TT_[2][:], TT_[1][:], NEGLAM, TT_[0][:], ALU.mult, ALU.add, ["T1", "T0", "LT"], ["T2"])
                                Pq, pqn = nextP()
                                ACT(Pq[:], TT_[2][:], AF.Square, ["T2"], [pqn])
                                MM(PS[7][:], ONESB[:], Pq[:], True, True, ["ONESB", pqn], [PSN[7]])
                                ACT(TT_[3][:], PS[7][:], AF.Sqrt, [PSN[7]], ["T3"], bias=EPS, scale=1.0 / 128.0)
                                RECIP(TT_[3][:], TT_[3][:], ["T3"], ["T3"])
                                STT("dve", R1[:, 16 + h, :], TT_[2][:], HG[:, h:h + 1], TT_[3][:], ALU.mult, ALU.mult, ["T2", "HG", "T3"], [r1(16 + h)])
                            steps.append((sc, po))
                run_pipeline(steps)
            add(da_all)

            if mix_stop == "da":
                return

            for m in range(8):
                def f(W, m=m):
                    Wv = wview(W)
                    for k in range(8):
                        MM(PS[0][:], Wv[:, 0, k, :], R1[:, 16 + k, :], k == 0, k == 7, [Wv.name_, r1(16 + k)], [PSN[0]])
                    for k in range(8):
                        MM(PS[1][:], Wv[:, 1, k, :], R2[:, 8 + k, :], k == 0, k == 7, [Wv.name_, r2(8 + k)], [PSN[1]])
                    for k in range(8):
                        MM(PS[2][:], Wv[:, 2, k, :], HT[:, k, :], k == 0, k == 7, [Wv.name_, "HT%d" % k], [PSN[2]])
                    for k in range(8):
                        MM(PS[3][:], Wv[:, 3, k, :], HT[:, k, :], k == 0, k == 7, [Wv.name_, "HT%d" % k], [PSN[3]])
                    ACT(TT_[0][:], PS[2][:], AF.Sigmoid, [PSN[2]], ["T0"])
                    ACT(TT_[1][:], PS[3][:], AF.Sigmoid, [PSN[3]], ["T1"])
                    TT("dve", TT_[0][:], PS[0][:], TT_[0][:], ALU.mult, [PSN[0], "T0"], ["T0"])
                    TT("dve", TT_[1][:], PS[1][:], TT_[1][:], ALU.mult, [PSN[1], "T1"], ["T1"])
                    TT("pool", R1[:, 8 + m, :], TT_[0][:], TT_[1][:], ALU.add, ["T0", "T1"], [r1(8 + m)])
                add_w("wm", m, 4096, f)
            for half in range(2):
                def f(W, half=half):
                    Wv = W[:, 0:4096].rearrange("p (k c) -> p k c", k=8)
                    for ml in range(4):
                        m = 4 * half + ml
                        py, pyn = PS[4 + ml % 2], PSN[4 + ml % 2]
                        for k in range(8):
                            MM(py[:], Wv[:, k, ml * 128:(ml + 1) * 128], R1[:, 8 + k, :], k == 0, k == 7, [W.name_, r1(8 + k)], [pyn])
                        TT("dve", XT[:, m, :], py[:], XT[:, m, :], ALU.add, [pyn, "XT%d" % m], ["XT%d" % m])
                add_w("wo", half, 4096, f)

        def load_x(s, i):
            def fn():
                c0 = i * TS_
                DMA("sp", XIN[:], x_d[s, c0:c0 + TS_, :].rearrange("(a p) f -> p a f", p=128), (), ["XIN"], "XIN")
                for c in range(8):
                    pp, ppn = PS[c % 2], PSN[c % 2]
                    for sub in range(4):
                        TR(pp[:, sub * 128:(sub + 1) * 128], XIN[:, sub, c * 128:(c + 1) * 128], IDENT[:], ["XIN", "IDENT"], [ppn])
                    CP("act" if c % 2 else "dve", XT[:, c, :], pp[:], [ppn], ["XT%d" % c])
            add(fn)

        def store_out(s, i):
            def fn():
                c0 = i * TS_
                for c in range(8):
                    P_, pn = nextP()
                    ACT(P_[:], XT[:, c, :], AF.Square, ["XT%d" % c], [pn])
                    MM(PS[6][:], ONESB[:], P_[:], c == 0, c == 7, [pn, "ONESB"], [PSN[6]])
                ACT(TT_[9][:], PS[6][:], AF.Sqrt, [PSN[6]], ["T9_0", "T9_1"], bias=EPS, scale=1.0 / D)
                RECIP(TT_[9][:], TT_[9][:], ["T9_0", "T9_1"], ["T9_0", "T9_1"])
                for c in range(8):
                    STT("dve", XT[:, c, :], XT[:, c, :], G4[:, 3, c:c + 1], TT_[9][:], ALU.mult, ALU.mult,
                        ["XT%d" % c, "G4", "T9_0", "T9_1"], ["XT%d" % c])
                for sub in range(4):
                    for cc in range(2):
                        pp, ppn = PS[cc], PSN[cc]
                        for cl in range(4):
                            c = 4 * cc + cl
                            TR(pp[:, cl * 128:(cl + 1) * 128], XT[:, c, sub * 128:(sub + 1) * 128], IDENT[:], ["XT%d" % c, "IDENT"], [ppn])
                        CP("act" if cc else "dve", XIN[:, sub, cc * 512:(cc + 1) * 512], pp[:], [ppn], ["XIN"])
                DMA("sp", out_d[s, c0:c0 + TS_, :].rearrange("(a p) f -> p a f", p=128), XIN[:], ["XIN"], ["OUT"], "OUT")
            add(fn)

        for s in range(NSEQ):
            for i in range(NT):
                load_x(s, i)
                if stop_after != "load":
                    ffn_items("ffn1a", "ffn1b", 0)
                    if stop_after != "ffn1":
                        mixer_items(s, i)
                        mixc["on"] = False
                        if stop_after != "mix":
                            ffn_items("ffn2a", "ffn2b", 2)
                store_out(s, i)

        widx = [k for k, it in enumerate(items) if it[0] is not None]
        state = {"next": 0}

        def issue_upto(n):
            while state["next"] <= n and state["next"] < len(widx):
                q = state["next"]
                key, ci, nelem, _ = items[widx[q]]
                slot = q % 3
                DMA("sp", WR[slot][:, 0:nelem], wdst[key][ci][:, 0:nelem], ["S_" + key], ["WR%d" % slot], "WR%d" % slot)
                state["next"] += 1

        wq = 0
        for k, it in enumerate(items):
            key, ci, nelem, fn = it
            if key is None:
                fn()
            else:
                issue_upto(wq + 2)
                slot = wq % 3
                fn(_WProxy(WR[slot], "WR%d" % slot))
                wq += 1

        em.wait_all("sp", ["OUT"])
        em.emit()
    return nc


class _WProxy:
    def __init__(self, t, name):
        self.t = t
        self.name_ = name

    def __getitem__(self, key):
        return _APProxy(self.t[key], self.name_)


class _APProxy:
    def __init__(self, ap, name):
        self.ap = ap
        self.name_ = name

    def rearrange(self, *a, **k):
        return _APProxy(self.ap.rearrange(*a, **k), self.name_)

    def __getitem__(self, key):
        return self.ap[key]


_CACHE = {}


def kernel(**inputs):
    x = np.asarray(inputs["x"], np.float32)
    B = x.shape[0]
    nseq = B // N_CORES
    w = prep_weights(inputs)
    c = make_consts()
    key = ("prog", nseq)
    if key not in _CACHE:
        _CACHE[key] = build_program(NSEQ=nseq, NT=4)
    nc = _CACHE[key]
    shared = {}
    shared.update(w)
    shared.update(c)
    in_maps = []
    for core in range(N_CORES):
        m = dict(shared)
        m["x"] = np.ascontiguousarray(x[core * nseq:(core + 1) * nseq])
        in_maps.append(m)
    res = run_bass_kernel_spmd(nc, in_maps, core_ids=list(range(N_CORES)))
    out = np.concatenate([r["out"] for r in res.results], axis=0)
    return out.astype(np.float32)
```

```python
import math
from contextlib import ExitStack
import numpy as np
import concourse.bass as bass
import concourse.mybir as mybir
from concourse.bass_utils import run_bass_kernel_spmd

F32 = mybir.dt.float32
BF16 = mybir.dt.bfloat16
AF = mybir.ActivationFunctionType
ALU = mybir.AluOpType
AX = mybir.AxisListType

T = 2048
D = 1024
DFF = 2816
NJ = 22
TS_ = 512
NEG = -30000.0
EPS = 1e-6
LAM_INIT = 0.8 - 0.6 * math.exp(0.0)
N_CORES = 8


class _Instr:
    __slots__ = ("eng", "fn", "deps", "sig", "is_dma", "idx", "waits")


class Emitter:
    ENGS = ("pe", "act", "dve", "pool", "sp")

    def __init__(self, nc):
        self.nc = nc
        self.streams = {e: [] for e in self.ENGS}
        self.state = {}
        self.dma_count = {}
        self.final = []

    def op(self, eng, fn, reads=(), writes=(), dma=None):
        ins = _Instr()
        ins.eng = eng
        ins.fn = fn
        ins.is_dma = dma is not None
        deps = {}

        def add(sig):
            if sig is None:
                return
            k, v = sig
            if deps.get(k, -1) < v:
                deps[k] = v

        for b in reads:
            st = self.state.get(b)
            if st is not None:
                add(st[0])
        for b in writes:
            st = self.state.get(b)
            if st is not None:
                add(st[0])
                for k, v in st[1].items():
                    add((k, v))
        stream = self.streams[eng]
        ins.idx = len(stream)
        if dma is not None:
            c = self.dma_count.get(dma, 0) + 1
            self.dma_count[dma] = c
            ins.sig = (("dma", dma), c)
        else:
            ins.sig = (eng, ins.idx)
        if eng == "pe":
            deps.pop("pe", None)
        ins.deps = deps
        stream.append(ins)
        k, v = ins.sig
        for b in reads:
            st = self.state.get(b)
            if st is None:
                st = [None, {}]
                self.state[b] = st
            if st[1].get(k, -1) < v:
                st[1][k] = v
        for b in writes:
            self.state[b] = [ins.sig, {}]
        return ins

    def wait_all(self, eng, bufs):
        self.final.append((eng, list(bufs)))

    def emit(self):
        nc = self.nc
        needed = {e: set() for e in self.ENGS}
        finals = {e: {} for e in self.ENGS}
        for eng, bufs in self.final:
            for b in bufs:
                st = self.state.get(b)
                if st is not None and st[0] is not None:
                    k, v = st[0]
                    if finals[eng].get(k, -1) < v:
                        finals[eng][k] = v
        for e, stream in self.streams.items():
            waited = {}
            for ins in stream:
                ins.waits = []
                for k, v in ins.deps.items():
                    if waited.get(k, -1) >= v:
                        continue
                    waited[k] = v
                    ins.waits.append((k, v))
                    if not isinstance(k, tuple):
                        needed[k].add(v)
            fw = []
            for k, v in finals[e].items():
                if waited.get(k, -1) >= v:
                    continue
                fw.append((k, v))
                if not isinstance(k, tuple):
                    needed[k].add(v)
            finals[e] = fw
        semval = {}
        for e, stream in self.streams.items():
            arr = [0] * len(stream)
            c = 0
            nd = needed[e]
            for i in range(len(stream)):
                if i in nd:
                    c += 1
                arr[i] = c
            semval[e] = arr
        with ExitStack() as es:
            sems = {}
            for e in self.ENGS:
                sems[e] = es.enter_context(nc.semaphore("s_" + e))
            for slot in self.dma_count:
                sems[("dma", slot)] = es.enter_context(nc.semaphore("d_" + str(slot)))
            block = es.enter_context(nc.Block())

            def run(ename, eng):
                nd = needed[ename]
                for ins in self.streams[ename]:
                    for k, v in ins.waits:
                        if isinstance(k, tuple):
                            eng.wait_ge(sems[k], 16 * v)
                        else:
                            eng.wait_ge(sems[k], semval[k][v])
                    bi = ins.fn(eng)
                    if ins.is_dma:
                        bi.then_inc(sems[ins.sig[0]], 16)
                    elif ins.idx in nd:
                        bi.then_inc(sems[ename], 1)
                for k, v in finals[ename]:
                    if isinstance(k, tuple):
                        eng.wait_ge(sems[k], 16 * v)
                    else:
                        eng.wait_ge(sems[k], semval[k][v])

            @block.tensor
            def _(eng):
                run("pe", eng)

            @block.scalar
            def _(eng):
                run("act", eng)

            @block.vector
            def _(eng):
                run("dve", eng)

            @block.gpsimd
            def _(eng):
                run("pool", eng)

            @block.sync
            def _(eng):
                run("sp", eng)


_PERM = np.array([p + 32 if (p % 64) < 32 else p - 32 for p in range(128)])


def _fm_chunks():
    ch = []
    for h in range(8):
        c = h * 128 + np.arange(128)
        ch += [c, c[_PERM]]
    for h in range(8):
        c = 1024 + h * 128 + np.arange(128)
        ch += [c, c[_PERM]]
    for j in range(8):
        c = np.concatenate([3072 + (0 * 8 + j) * 64 + np.arange(64), 3072 + (8 + j) * 64 + np.arange(64)])
        ch += [c, c[_PERM]]
    ks = 4352 + np.arange(128)
    kw = 4608 + np.arange(128)
    ch += [ks, ks[_PERM], kw, kw[_PERM]]
    raw = [np.concatenate([4096 + g * 64 + np.arange(64), 4224 + g * 64 + np.arange(64)]) for g in range(2)]
    ch += [raw[0], raw[1], raw[0], raw[1]]

    def gate(j, b):
        return np.concatenate([np.full(64, 4864 + (0 * 8 + j) * 3 + b), np.full(64, 4864 + (8 + j) * 3 + b)])

    for j in range(8):
        ch.append(gate(j, 0))
    for j in range(8):
        ch += [gate(j, 1), gate(j, 2)]
    assert len(ch) == 80
    return ch


def _lay_kp(w, cols):
    return np.ascontiguousarray(w[:, cols].reshape(8, 128, len(cols)).transpose(1, 0, 2))


def prep_weights(inp):
    f = np.float32
    o = {}
    for nm in ("ffn1", "ffn2"):
        w1 = np.asarray(inp[nm + "_w1"][0], f)
        w3 = np.asarray(inp[nm + "_w3"][0], f)
        w2 = np.asarray(inp[nm + "_w2"][0], f)
        a = w1.reshape(8, 128, 11, 256).transpose(2, 1, 0, 3)
        b = w3.reshape(8, 128, 11, 256).transpose(2, 1, 0, 3)
        o[nm + "a"] = np.ascontiguousarray(np.stack([a, b], axis=2)).reshape(11, 128, 4096)
        o[nm + "b"] = np.ascontiguousarray(w2.reshape(22, 128, 8, 128).transpose(2, 1, 0, 3)).reshape(8, 128, 2816)
    win = np.asarray(inp["w_in"][0], f)
    ch = _fm_chunks()
    wf = np.stack([_lay_kp(win, c) for c in ch], axis=0)
    o["wf"] = np.ascontiguousarray(wf.reshape(20, 4, 128, 8, 128).transpose(0, 2, 1, 3, 4)).reshape(20, 128, 4096)
    wt = [_lay_kp(win, 2048 + h * 512 + np.arange(512)).reshape(128, 4096) for h in range(2)]
    vsw = _lay_kp(win, np.concatenate([4480 + np.arange(128), 4736 + np.arange(128)])).reshape(128, 2048)
    vsw = np.concatenate([vsw, np.zeros((128, 2048), f)], axis=1)
    o["wt"] = np.ascontiguousarray(np.stack(wt + [vsw], axis=0))
    c1 = np.asarray(inp["cmp_w1"][0], f)
    c1 = c1.reshape(2, 32, 64, 256).transpose(0, 2, 1, 3).reshape(128, 32, 256)
    o["w1c"] = np.ascontiguousarray(c1.reshape(128, 2, 16 * 256).transpose(1, 0, 2))
    wpd = np.asarray(inp["w_proj_da"][0], f)
    wpn = np.asarray(inp["w_proj_nsa"][0], f)
    wo = np.asarray(inp["w_out"][0], f)
    rows_n = np.array([[((p // 64) * 8 + j) * 64 + p % 64 for p in range(128)] for j in range(8)])
    wm = np.zeros((8, 128, 4, 8, 128), f)
    for m in range(8):
        cs = m * 128 + np.arange(128)
        wm[m, :, 0] = _lay_kp(wpd, cs)
        wm[m, :, 1] = wpn[rows_n.T, :][:, :, cs]
        wm[m, :, 2] = _lay_kp(win, 4912 + cs)
        wm[m, :, 3] = _lay_kp(win, 5936 + cs)
    o["wm"] = wm.reshape(8, 128, 4096)
    o["wo"] = np.ascontiguousarray(np.stack([_lay_kp(wo, h * 512 + np.arange(512)).reshape(128, 4096) for h in range(2)], 0))
    g4 = np.stack([np.asarray(inp[k], f).reshape(-1) for k in ("ffn1_norm", "mix_norm", "ffn2_norm", "final_norm")], 0)
    o["g4"] = np.ascontiguousarray(g4.reshape(4, 8, 128).transpose(2, 0, 1)).reshape(128, 32)
    o["hg"] = np.ascontiguousarray(np.asarray(inp["da_head_norm"][0], f).T)
    o["lamb"] = np.asarray(inp["da_lambda"][0], f).reshape(1, 256)
    o["pos"] = np.ascontiguousarray(np.asarray(inp["cmp_pos"][0], f).transpose(0, 2, 1).reshape(128, 32))
    w2c = np.asarray(inp["cmp_w2"][0], f)
    w2k = np.zeros((128, 2, 2, 128), f)
    for hc in range(2):
        for g in range(2):
            w2k[:, hc, g, 64 * g:64 * g + 64] = w2c[0, hc * 128:(hc + 1) * 128, :]
    o["w2k"] = w2k.reshape(128, 512)
    o["w2v"] = np.ascontiguousarray(w2c[1].reshape(2, 128, 64).transpose(1, 0, 2)).reshape(128, 128)
    return o


def make_consts():
    f = np.float32
    c = {}
    pos = np.arange(T, dtype=f)
    inv = (1.0 / (10000.0 ** (np.arange(0, 64, 2, dtype=f) / f(64)))).astype(f)
    ang = (pos[:, None] * inv[None, :]).astype(f)
    cos = np.cos(ang).astype(f).T
    sin = np.sin(ang).astype(f).T
    p = np.arange(128)
    sgn = np.where((p % 64) < 32, -1.0, 1.0).astype(f)
    c["cos"] = np.ascontiguousarray(cos[p % 32, :])
    c["sins"] = np.ascontiguousarray(sin[p % 32, :] * sgn[:, None])
    kl = np.arange(128)[:, None]
    ql = np.arange(512)[None, :]
    cm = np.stack([np.where(128 * r + kl <= ql, 0.0, NEG) for r in range(4)], 1)
    wm = np.stack([np.where(128 * r + kl > ql, 0.0, NEG) for r in range(4)], 1)
    c["cmwm"] = np.concatenate([cm, wm], 1).astype(f).reshape(128, 8 * 512)
    n = np.arange(128)[:, None]
    t = np.arange(T)[None, :]
    c["cmask"] = np.where((16 * n + 31 <= t) & (n <= 126), 0.0, NEG).astype(f)
    s = np.arange(32)[:, None]
    xx = np.arange(T)[None, :]
    c["emat"] = (xx // 64 == s).astype(f)
    c["ident"] = np.eye(128, dtype=f)
    ov = np.zeros((128, 64), f)
    for nn in range(127):
        for ss in range(32):
            if nn * 16 < ss * 64 + 64 and nn * 16 + 32 > ss * 64:
                ov[nn, ss] = 1.0
    ov[:, 32:] = 1.0
    c["ov"] = ov
    tt = np.arange(T)
    blk = tt // 64
    sb = np.arange(32)[None, :]
    forced = (sb == 0) | (sb == blk[:, None]) | (sb == blk[:, None] - 1)
    causal = sb <= blk[:, None]
    caus = (causal & ~forced).astype(f)
    addc = np.where(forced, 1e9, np.where(causal, 0.0, -1.0)).astype(f)
    topc = np.stack([caus, addc], 1)
    c["topc"] = np.ascontiguousarray(topc.reshape(16, 128, 2, 32).transpose(1, 0, 2, 3)).reshape(128, 16 * 64)
    return c


def build_program(NSEQ=4, NT=4, stop_after=None, mix_stop=None):
    nc = bass.Bass("TRN2", target_bir_lowering=False)
    em = Emitter(nc)

    def din(name, shape, dt=F32):
        return nc.dram_tensor(name, list(shape), dt, kind="ExternalInput").ap()

    def dscr(name, shape, dt=BF16):
        return nc.dram_tensor(name, list(shape), dt, kind="Internal").ap()

    x_d = din("x", [NSEQ, T, D])
    out_d = nc.dram_tensor("out", [NSEQ, T, D], F32, kind="ExternalOutput").ap()
    wsrc = {}
    wdst = {}
    wshapes = {"ffn1a": [11, 128, 4096], "ffn1b": [8, 128, 2816], "wf": [20, 128, 4096], "wt": [3, 128, 4096],
               "w1c": [2, 128, 4096], "wm": [8, 128, 4096], "wo": [2, 128, 4096],
               "ffn2a": [11, 128, 4096], "ffn2b": [8, 128, 2816]}
    for k, shp in wshapes.items():
        wsrc[k] = din(k, shp)
        wdst[k] = dscr("s_" + k, shp)
    g4_d = din("g4", [128, 32])
    hg_d = din("hg", [128, 8])
    lamb_d = din("lamb", [1, 256])
    pos_d = din("pos", [128, 32])
    w2k_d = din("w2k", [128, 512])
    w2v_d = din("w2v", [128, 128])
    cos_d = din("cos", [128, T])
    sins_d = din("sins", [128, T])
    cmwm_d = din("cmwm", [128, 4096])
    cmask_d = din("cmask", [128, T])
    emat_d = din("emat", [32, T])
    ident_d = din("ident", [128, 128])
    ov_d = din("ov", [128, 64])
    topc_d = din("topc", [128, 1024])
    kda_d = dscr("kda_scr", [8, 128, T])
    vda_d = dscr("vda_scr", [8, 128, 16, 128])

    es = ExitStack()
    with es:
        def sb(name, shape, dt):
            return es.enter_context(nc.sbuf_tensor(name, list(shape), dt))

        PS = [es.enter_context(nc.psum_tensor("ps%d" % i, [128, 512], F32)) for i in range(8)]
        PSN = ["ps%d" % i for i in range(8)]

        XT = sb("XT", [128, 8, 512], F32)
        HT = sb("HT", [128, 8, 512], BF16)
        R1 = sb("R1", [128, 24, 512], BF16)
        R2 = sb("R2", [128, 16, 512], BF16)
        KsT = sb("KsT", [128, T], BF16)
        KwT = sb("KwT", [128, T], BF16)
        VsA = sb("VsA", [128, 16, 2, 128], BF16)
        VwA = sb("VwA", [128, 16, 2, 128], BF16)
        GG = sb("GG", [128, 2, 2, 2, 128], BF16)
        KcT = sb("KcT", [128, 128], BF16)
        VcA = sb("VcA", [128, 2, 128], BF16)
        RAW = sb("RAW", [128, 2, 528], BF16)
        KD = [sb("KD%d" % i, [128, T], BF16) for i in range(2)]
        VD = [sb("VD%d" % i, [128, 16, 128], BF16) for i in range(2)]
        WR = [sb("WR%d" % i, [128, 4096], BF16) for i in range(3)]
        TT_ = [sb("T%d" % i, [128, 512], F32) for i in range(10)]
        PB = [sb("P%d" % i, [128, 512], BF16) for i in range(4)]
        COS = sb("COS", [128, 512], F32)
        SINS = sb("SINS", [128, 512], F32)
        CMWM = sb("CMWM", [128, 8, 512], BF16)
        CMASK = sb("CMASK", [128, 512], BF16)
        EMAT = sb("EMAT", [128, T], BF16)
        IDENT = sb("IDENT", [128, 128], F32)
        IDENTB = sb("IDENTB", [128, 128], BF16)
        ONESB = sb("ONESB", [128, 128], BF16)
        OV = sb("OV", [128, 64], BF16)
        TOPC = sb("TOPC", [128, 4, 2, 32], F32)
        G4 = sb("G4", [128, 4, 8], F32)
        HG = sb("HG", [128, 8], F32)
        LAMB = sb("LAMB", [128, 256], F32)
        LT = sb("LT", [128, 8], F32)
        POS = sb("POS", [128, 32], BF16)
        W2K = sb("W2K", [128, 2, 2, 128], BF16)
        W2V = sb("W2V", [128, 2, 64], BF16)
        CB = sb("CB", [128, 4], F32)
        SELBT = sb("SELBT", [128, 512], BF16)
        IMPACC = [sb("IMPACC%d" % g, [32, 512], F32) for g in range(2)]
        SC = sb("SC", [128, 4, 32], F32)
        SC2 = sb("SC2", [128, 4, 32], F32)
        SELB = sb("SELB", [128, 4, 32], F32)
        M8a = sb("M8a", [128, 8], F32)
        M8b = sb("M8b", [128, 8], F32)
        GEL = [sb("GEL%d" % i, [128, 256], F32) for i in range(3)]
        XIN = sb("XIN", [128, 4, 1024], F32)

        try:
            print("SBUF bytes remaining:", nc.sbuf_bytes_remaining)
        except Exception as ex:
            print("sbuf_bytes_remaining failed", ex)
        def MM(out, lhsT, rhs, start, stop, r, w):
            em.op("pe", lambda e: e.matmul(out, lhsT, rhs, start=start, stop=stop), r, w)

        def TR(out, in_, ident, r, w):
            em.op("pe", lambda e: e.transpose(out, in_, ident), r, w)

        def ACT(out, in_, func, r, w, bias=None, scale=None):
            kw = {}
            if bias is not None:
                kw["bias"] = bias
            if scale is not None:
                kw["scale"] = scale
            em.op("act", lambda e: e.activation(out, in_, func, **kw), r, w)

        def TT(eng, out, in0, in1, op, r, w):
            em.op(eng, lambda e: e.tensor_tensor(out, in0, in1, op), r, w)

        def TSC(eng, out, in0, s1, s2, op0, op1, r, w):
            if op1 is None:
                em.op(eng, lambda e: e.tensor_scalar(out, in0, s1, None, op0), r, w)
            else:
                em.op(eng, lambda e: e.tensor_scalar(out, in0, s1, s2, op0, op1), r, w)

        def STT(eng, out, in0, scalar, in1, op0, op1, r, w):
            em.op(eng, lambda e: e.scalar_tensor_tensor(out, in0, scalar, in1, op0, op1), r, w)

        def CP(eng, out, in_, r, w):
            if eng == "act":
                em.op("act", lambda e: e.activation(out, in_, AF.Copy), r, w)
            else:
                em.op(eng, lambda e: e.tensor_copy(out, in_), r, w)

        def RECIP(out, in_, r, w):
            em.op("dve", lambda e: e.reciprocal(out, in_), r, w)

        def MEMSET(eng, ap, val, w):
            em.op(eng, lambda e: e.memset(ap, val), (), w)

        def DMA(eng, out, in_, r, w, slot):
            em.op(eng, lambda e: e.dma_start(out=out, in_=in_), r, w, dma=slot)

        def r1(u):
            return "R1_%d" % u

        def r2(u):
            return "R2_%d" % u

        cnt = {"p": 0, "s": 0, "t": 0}

        def nextP():
            k = cnt["p"] % 4
            cnt["p"] += 1
            return PB[k], "P%d" % k

        def nextS():
            k = cnt["s"] % 3
            cnt["s"] += 1
            return PS[k], PSN[k]

        for k in ("ffn1a", "ffn1b", "wf", "wt", "w1c", "wm", "wo", "ffn2a", "ffn2b"):
            n = wshapes[k][0]
            for q in range(n):
                last = q == n - 1
                DMA("pool", wdst[k][q], wsrc[k][q], (), ["S_" + k if last else "S_%s_%d" % (k, q)], "cast_" + k)

        def cload(dst_ap, src_ap, name, eng="sp"):
            DMA(eng, dst_ap, src_ap, (), [name], "c_" + name)

        cload(IDENT[:], ident_d, "IDENT")
        cload(G4[:].rearrange("p a b -> p (a b)"), g4_d, "G4")
        cload(HG[:], hg_d, "HG")
        cload(LAMB[:], lamb_d.broadcast_to([128, 256]), "LAMB")
        cload(IDENTB[:], ident_d, "IDENTB", "pool")
        cload(CMWM[:].rearrange("p a b -> p (a b)"), cmwm_d, "CMWM", "pool")
        cload(EMAT[0:32, :], emat_d, "EMAT", "pool")
        cload(EMAT[64:96, :], emat_d, "EMATb", "pool")
        cload(OV[:], ov_d, "OV", "pool")
        cload(POS[:], pos_d, "POS", "pool")
        cload(W2K[:].rearrange("p a b c -> p (a b c)"), w2k_d, "W2K", "pool")
        cload(W2V[:].rearrange("p a b -> p (a b)"), w2v_d, "W2V", "pool")
        MEMSET("pool", ONESB[:], 1.0, ["ONESB"])
        MEMSET("pool", VsA[:, :, :, 64:128], 1.0, ["VsA1"])
        MEMSET("pool", VwA[:, :, :, 64:128], 1.0, ["VwA1"])
        MEMSET("pool", VcA[:, :, 64:128], 1.0, ["VcA1"])
        TT("dve", LAMB[:, 0:64], LAMB[:, 0:64], LAMB[:, 64:128], ALU.mult, ["LAMB"], ["LAMB"])
        TT("dve", LAMB[:, 128:192], LAMB[:, 128:192], LAMB[:, 192:256], ALU.mult, ["LAMB"], ["LAMB"])
        em.op("dve", lambda e: e.reduce_sum(LT[:, 0:1], LAMB[:, 0:64], AX.X), ["LAMB"], ["LT"])
        em.op("dve", lambda e: e.reduce_sum(LT[:, 1:2], LAMB[:, 128:192], AX.X), ["LAMB"], ["LT"])
        ACT(LT[:, 2:4], LT[:, 0:2], AF.Exp, ["LT"], ["LT"])
        TT("dve", LT[:, 4:5], LT[:, 3:4], LT[:, 2:3], ALU.subtract, ["LT"], ["LT"])
        TSC("dve", LT[:, 4:5], LT[:, 4:5], -LAM_INIT, None, ALU.add, None, ["LT"], ["LT"])
        TSC("dve", HG[:], HG[:], 1.0 - LAM_INIT, None, ALU.mult, None, ["HG"], ["HG"])
        NEGLAM = LT[:, 4:5]

        items = []

        import os
        MIXN = int(os.environ.get("MIXN", "100000"))
        mixc = {"n": 0, "on": False}

        def add_w(key, q, nelem, fn):
            if mixc["on"]:
                mixc["n"] += 1
                if mixc["n"] > MIXN:
                    return
            items.append((key, q, nelem, fn))

        def add(fn):
            items.append((None, None, None, fn))

        def cb_item(half):
            def fn(W):
                Wv = W[:, 0:4096].rearrange("p (l h) -> p l h", h=256)
                for which in range(2):
                    pacc, paccn = (PS[7], PSN[7]) if which == 0 else (PS[6], PSN[6])
                    for hc in range(2):
                        col = half * 2 + hc
                        for ll in range(16):
                            l = half * 16 + ll
                            MM(pacc[:, col:col + 1], Wv[64 * which:64 * which + 64, ll, hc * 128:(hc + 1) * 128],
                               POS[64 * which:64 * which + 64, l:l + 1], ll == 0, ll == 15, [W.name_, "POS"], [paccn])
                if half == 1:
                    for which in range(2):
                        pacc, paccn = (PS[7], PSN[7]) if which == 0 else (PS[6], PSN[6])
                        CP("dve", CB[:, which * 2:which * 2 + 2], pacc[:, 0:2], [paccn], ["CB"])
                        TT("dve", CB[:, which * 2:which * 2 + 2], CB[:, which * 2:which * 2 + 2], pacc[:, 2:4], ALU.add, [paccn, "CB"], ["CB"])
            return fn

        add_w("w1c", 0, 4096, cb_item(0))
        add_w("w1c", 1, 4096, cb_item(1))

        def rmsnorm_to_HT(gi):
            for c in range(8):
                P_, pn = nextP()
                ACT(P_[:], XT[:, c, :], AF.Square, ["XT%d" % c], [pn])
                MM(PS[6][:], ONESB[:], P_[:], c == 0, c == 7, [pn, "ONESB"], [PSN[6]])
            ACT(TT_[9][:], PS[6][:], AF.Sqrt, [PSN[6]], ["T9_0", "T9_1"], bias=EPS, scale=1.0 / D)
            RECIP(TT_[9][:], TT_[9][:], ["T9_0", "T9_1"], ["T9_0", "T9_1"])
            for c in range(8):
                STT("dve", HT[:, c, :], XT[:, c, :], G4[:, gi, c:c + 1], TT_[9][:], ALU.mult, ALU.mult,
                    ["XT%d" % c, "G4", "T9_0", "T9_1"], ["HT%d" % c])

        def ffn_items(key_a, key_b, gi):
            def pre():
                rmsnorm_to_HT(gi)
            add(pre)
            for jj in range(11):
                def fa(W, jj=jj):
                    Wv = W[:, 0:4096].rearrange("p (a k c) -> p a k c", a=2, k=8)
                    for jl in range(2):
                        j = 2 * jj + jl
                        pa, pan = PS[j % 2], PSN[j % 2]
                        pb, pbn = PS[2 + j % 2], PSN[2 + j % 2]
                        for k in range(8):
                            MM(pa[:], Wv[:, 0, k, jl * 128:(jl + 1) * 128], HT[:, k, :], k == 0, k == 7,
                               [W.name_, "HT%d" % k], [pan])
                        for k in range(8):
                            MM(pb[:], Wv[:, 1, k, jl * 128:(jl + 1) * 128], HT[:, k, :], k == 0, k == 7,
                               [W.name_, "HT%d" % k], [pbn])
                        tk = j % 2
                        ACT(TT_[tk][:], pa[:], AF.Silu, [pan], ["T%d" % tk])
                        TT("dve", R1[:, j, :], TT_[tk][:], pb[:], ALU.mult, ["T%d" % tk, pbn], [r1(j)])
                add_w(key_a, jj, 4096, fa)
            for m in range(8):
                def fb(W, m=m):
                    Wv = W[:, 0:2816].rearrange("p (j c) -> p j c", c=128)
                    py, pyn = PS[4 + m % 2], PSN[4 + m % 2]
                    for j in range(NJ):
                        MM(py[:], Wv[:, j, :], R1[:, j, :], j == 0, j == NJ - 1, [W.name_, r1(j)], [pyn])
                    STT("dve", XT[:, m, :], py[:], 0.5, XT[:, m, :], ALU.mult, ALU.add, [pyn, "XT%d" % m], ["XT%d" % m])
                add_w(key_b, m, 2816, fb)

        def rope_evac(px, pxn, ps_, psn, out_ap, w):
            ta, tb = cnt["t"] % 2, 2 + cnt["t"] % 2
            cnt["t"] += 1
            TT("dve", TT_[ta][:], px[:], COS[:], ALU.mult, [pxn, "COS"], ["T%d" % ta])
            TT("dve", TT_[tb][:], ps_[:], SINS[:], ALU.mult, [psn, "SINS"], ["T%d" % tb])
            TT("pool", out_ap, TT_[ta][:], TT_[tb][:], ALU.add, ["T%d" % ta, "T%d" % tb], w)

        def mixer_items(s, i):
            mixc["on"] = True
            mixc["n"] = 0
            c0 = i * TS_
            tt0 = 4 * i

            def pre():
                rmsnorm_to_HT(1)
                DMA("sp", COS[:], cos_d[:, c0:c0 + TS_], (), ["COS"], "COS")
                DMA("sp", SINS[:], sins_d[:, c0:c0 + TS_], (), ["SINS"], "SINS")
                DMA("pool", CMASK[:], cmask_d[:, c0:c0 + TS_], (), ["CMASK"], "CMASK")
                DMA("sp", TOPC[:].rearrange("p a b c -> p (a b c)"), topc_d[:, i * 256:(i + 1) * 256], (), ["TOPC"], "TOPC")
                if i == 0:
                    MEMSET("pool", GG[:].rearrange("p a b c d -> p (a b c d)"), 0.0, ["GG"])
            add(pre)

            def proj_pair(Wv, q, k0=8):
                a = (cnt["s"] % 2) * 2
                cnt["s"] += 1
                px, pxn = PS[a], PSN[a]
                ps_, psn = PS[a + 1], PSN[a + 1]
                for k in range(8):
                    MM(px[:], Wv[:, 2 * q, k, :], HT[:, k, :], k == 0, k == 7, [Wv.name_, "HT%d" % k], [pxn])
                for k in range(8):
                    MM(ps_[:], Wv[:, 2 * q + 1, k, :], HT[:, k, :], k == 0, k == 7, [Wv.name_, "HT%d" % k], [psn])
                return px, pxn, ps_, psn

            def wview(W):
                v = W[:, 0:4096].rearrange("p (q k c) -> p q k c", q=4, k=8)
                v.name_ = W.name_
                return v

            for gq in range(4):
                def f(W, gq=gq):
                    Wv = wview(W)
                    for q in range(2):
                        h = 2 * gq + q
                        px, pxn, ps_, psn = proj_pair(Wv, q)
                        rope_evac(px, pxn, ps_, psn, R1[:, h, :], [r1(h)])
                add_w("wf", gq, 4096, f)
            for gq in range(4):
                def f(W, gq=gq):
                    Wv = wview(W)
                    for q in range(2):
                        h = 2 * gq + q
                        px, pxn, ps_, psn = proj_pair(Wv, q)
                        rope_evac(px, pxn, ps_, psn, R2[:, h, :], [r2(h)])
                add_w("wf", 4 + gq, 4096, f)
            for gq in range(4):
                def f(W, gq=gq):
                    Wv = wview(W)
                    for q in range(2):
                        j = 2 * gq + q
                        px, pxn, ps_, psn = proj_pair(Wv, q)
                        if not os.environ.get("NOQN"):
                            CP(os.environ.get("QNENG", "dve"), R1[:, 16 + j, :], px[:], [pxn], [r1(16 + j)])
                        rope_evac(px, pxn, ps_, psn, R1[:, 8 + j, :], [r1(8 + j)])
                add_w("wf", 8 + gq, 4096, f)

            def f_kskw(W):
                Wv = wview(W)
                px, pxn, ps_, psn = proj_pair(Wv, 0)
                rope_evac(px, pxn, ps_, psn, KsT[:, c0:c0 + TS_], ["KsT%d" % i])
                px, pxn, ps_, psn = proj_pair(Wv, 1)
                rope_evac(px, pxn, ps_, psn, KwT[:, c0:c0 + TS_], ["KwT%d" % i])
            add_w("wf", 12, 4096, f_kskw)

            def f_raw(W):
                Wv = wview(W)
                for g in range(2):
                    pp, ppn = PS[4 + g], PSN[4 + g]
                    for k in range(8):
                        MM(pp[:], Wv[:, g, k, :], HT[:, k, :], k == 0, k == 7, [Wv.name_, "HT%d" % k], [ppn])
                    CP("act", RAW[:, g, 16:528], pp[:], [ppn], ["RAW%d" % g])
            add_w("wf", 13, 4096, f_raw)

            for half in range(2):
                def f(W, half=half):
                    Wv = W[:, 0:4096].rearrange("p (k c) -> p k c", k=8)
                    for sub in range(4):
                        pp, ppn = PS[4 + sub % 2], PSN[4 + sub % 2]
                        for k in range(8):
                            MM(pp[:], HT[:, k, sub * 128:(sub + 1) * 128], Wv[:, k, :], k == 0, k == 7,
                               [W.name_, "HT%d" % k], [ppn])
                        u = 8 + 2 * sub + half
                        CP("act" if sub % 2 else "dve", R2[:, u, :], pp[:], [ppn], [r2(u)])
                add_w("wt", half, 4096, f)

            def f_vsw(W):
                Wv = W[:, 0:2048].rearrange("p (k c) -> p k c", k=8)
                for sub in range(4):
                    pp, ppn = PS[4 + sub % 2], PSN[4 + sub % 2]
                    for k in range(8):
                        MM(pp[:, 0:256], HT[:, k, sub * 128:(sub + 1) * 128], Wv[:, k, :], k == 0, k == 7,
                           [W.name_, "HT%d" % k], [ppn])
                    CP("dve", VsA[:, tt0 + sub, :, 0:64], pp[:, 0:128].rearrange("p (g d) -> p g d", g=2), [ppn], ["VsA%d" % (tt0 + sub)])
                    CP("dve", VwA[:, tt0 + sub, :, 0:64], pp[:, 128:256].rearrange("p (g d) -> p g d", g=2), [ppn], ["VwA%d" % (tt0 + sub)])
                import os
                if os.environ.get("NOSPILL"):
                    return
                DMA("pool", kda_d[:, :, c0:c0 + TS_].rearrange("h p c -> p h c"), R2[:, 0:8, :],
                    [r2(u) for u in range(8)], ["KDAD%d" % i], "kst")
                for sub in range(4):
                    src = R2[:, 8 + 2 * sub:10 + 2 * sub, :].rearrange("p a (h e) -> p (a h) e", e=128)
                    DMA("pool", vda_d[:, :, tt0 + sub, :].rearrange("h p e -> p h e"), src,
                        [r2(8 + 2 * sub), r2(9 + 2 * sub)], ["VDAD%d_%d" % (i, sub)], "vst%d" % sub)
            add_w("wt", 2, 2048, f_vsw)

            if mix_stop == "proj":
                return
            nb = 31 if i == 0 else 32
            b0 = 1 if i == 0 else 0
            n0 = 0 if i == 0 else 32 * i - 1
            for half in range(2):
                def f(W, half=half):
                    Wv = W[:, 0:4096].rearrange("p (l h) -> p l h", h=256)
                    for which in range(2):
                        for g in range(2):
                            for hc in range(2):
                                combo = (which * 2 + g) * 2 + hc
                                pacc, paccn = (PS[6], PSN[6]) if which == 0 else (PS[5], PSN[5])
                                cc = half * 4 + g * 2 + hc
                                for ll in range(16):
                                    l = half * 16 + ll
                                    MM(pacc[:, cc * 32:cc * 32 + nb],
                                       Wv[64 * which:64 * which + 64, ll, hc * 128:(hc + 1) * 128],
                                       RAW[64 * which:64 * which + 64, g, l + 16 * b0:l + 16 * b0 + 16 * (nb - 1) + 1:16],
                                       ll == 0, ll == 15, [W.name_, "RAW%d" % g], [paccn])
                    if half == 1:
                        for which in range(2):
                            pacc, paccn = (PS[6], PSN[6]) if which == 0 else (PS[5], PSN[5])
                            for g in range(2):
                                for hc in range(2):
                                    combo = (which * 2 + g) * 2 + hc
                                    cc = g * 2 + hc
                                    col = which * 2 + hc
                                    ACT(GEL[0][:, combo * 32:combo * 32 + 32], pacc[:, cc * 32:cc * 32 + 32], AF.Identity,
                                        [paccn, "CB"], ["GEL0"], bias=CB[:, col:col + 1], scale=1.0)
                            TT("dve", GEL[0][:, which * 128:which * 128 + 128], GEL[0][:, which * 128:which * 128 + 128], pacc[:, 128:256],
                               ALU.add, ["GEL0", paccn], ["GEL0"])
                        TT("dve", GEL[1][:], GEL[0][:], GEL[0][:], ALU.mult, ["GEL0"], ["GEL1"])
                        TSC("dve", GEL[1][:], GEL[1][:], 0.044715, 1.0, ALU.mult, ALU.add, ["GEL1"], ["GEL1"])
                        TT("dve", GEL[1][:], GEL[1][:], GEL[0][:], ALU.mult, ["GEL1", "GEL0"], ["GEL1"])
                        ACT(GEL[2][:], GEL[1][:], AF.Tanh, ["GEL1"], ["GEL2"], scale=0.7978845608028654)
                        TSC("dve", GEL[2][:], GEL[2][:], 0.5, 0.5, ALU.mult, ALU.add, ["GEL2"], ["GEL2"])
                        for which in range(2):
                            for g in range(2):
                                for hc in range(2):
                                    combo = (which * 2 + g) * 2 + hc
                                    TT("dve", GG[:, which, hc, g, n0:n0 + nb], GEL[2][:, combo * 32:combo * 32 + nb],
                                       GEL[0][:, combo * 32:combo * 32 + nb], ALU.mult, ["GEL2", "GEL0"], ["GG"])
                        q = 0
                        for hc in range(2):
                            for g in range(2):
                                MM(PS[7][:, 0:128], W2K[:, hc, g, :], GG[:, 0, hc, g, :], q == 0, q == 3, ["W2K", "GG"], [PSN[7]])
                                q += 1
                        CP("dve", KcT[:], PS[7][:, 0:128], [PSN[7]], ["KcT"])
                        for g in range(2):
                            for hc in range(2):
                                MM(PS[7][:, 128 + 64 * g:192 + 64 * g], GG[:, 1, hc, g, :], W2V[:, hc, :], hc == 0, hc == 1,
                                   ["W2V", "GG"], [PSN[7]])
                        CP("dve", VcA[:, :, 0:64], PS[7][:, 128:256].rearrange("p (g d) -> p g d", g=2), [PSN[7]], ["VcA"])
                        if i < NT - 1:
                            for g in range(2):
                                CP("pool", RAW[:, g, 0:16], RAW[:, g, 512:528], ["RAW%d" % g], ["RAW%d" % g])
                add_w("w1c", half, 4096, f)

            if mix_stop == "compress":
                return

            def gate_tile(Wv, q, tk):
                for k in range(8):
                    MM(PS[7][:], Wv[:, q, k, :], HT[:, k, :], k == 0, k == 7, [Wv.name_, "HT%d" % k], [PSN[7]])
                ACT(TT_[tk][:], PS[7][:], AF.Sigmoid, [PSN[7]], ["T%d" % tk])

            for gq in range(2):
                def f(W, gq=gq):
                    Wv = wview(W)
                    for q in range(4):
                        j = 4 * gq + q
                        gt = 4 + j % 2
                        gate_tile(Wv, q, gt)
                        for g in range(2):
                            lo, hi = 64 * g, 64 * g + 64
                            S_, sn = nextS()
                            MM(S_[:], KcT[lo:hi, :], R1[lo:hi, 16 + j, :], True, False, ["KcT", r1(16 + j)], [sn])
                            MM(S_[:], IDENTB[:], CMASK[:], False, True, ["IDENTB", "CMASK"], [sn])
                            P_, pn = nextP()
                            ACT(P_[:], S_[:], AF.Exp, [sn], [pn], scale=0.125)
                            oc, ocn = PS[3 + g], PSN[3 + g]
                            MM(oc[:], VcA[:, g, :], P_[:], True, True, ["VcA", "VcA1", pn], [ocn])
                            tn = "T6_%d" % g
                            TSC("dve", TT_[6][lo:hi, :], oc[64:128, :], 1e-30, None, ALU.add, None, [ocn], [tn])
                            RECIP(TT_[6][lo:hi, :], TT_[6][lo:hi, :], [tn], [tn])
                            TT("dve", TT_[6][lo:hi, :], TT_[6][lo:hi, :], TT_[gt][lo:hi, :], ALU.mult, [tn, "T%d" % gt], [tn])
                            TT("dve", R2[lo:hi, j, :], oc[0:64, :], TT_[6][lo:hi, :], ALU.mult, [ocn, tn], [r2(j)])
                            if i >= 2:
                                im, imn = PS[5], PSN[5]
                                MM(im[0:64, :], OV[:], P_[:], True, True, ["OV", pn], [imn])
                                TSC("dve", TT_[9][32:64, :], im[32:64, :], 1e-30, None, ALU.add, None, [imn], ["T9_0"])
                                RECIP(TT_[9][32:64, :], TT_[9][32:64, :], ["T9_0"], ["T9_0"])
                                if j == 0:
                                    TT("dve", IMPACC[g][:], im[0:32, :], TT_[9][32:64, :], ALU.mult, [imn, "T9_0"], ["IMPACC%d" % g])
                                else:
                                    TT("dve", TT_[9][0:32, :], im[0:32, :], TT_[9][32:64, :], ALU.mult, [imn, "T9_0"], ["T9_0"])
                                    TT("pool", IMPACC[g][:], IMPACC[g][:], TT_[9][0:32, :], ALU.add, ["IMPACC%d" % g, "T9_0"], ["IMPACC%d" % g])
                add_w("wf", 14 + gq, 4096, f)

            if mix_stop == "cmp":
                return

            def topk():
                if i < 2:
                    return
                for g in range(2):
                    pt, ptn = PS[5], PSN[5]
                    for sub in range(4):
                        TR(pt[:, sub * 32:(sub + 1) * 32], IMPACC[g][0:32, sub * 128:(sub + 1) * 128], IDENT[0:32, 0:32],
                           ["IMPACC%d" % g, "IDENT"], [ptn])
                    TT("dve", SC[:], pt[:, 0:128].rearrange("p (s b) -> p s b", b=32), TOPC[:, :, 0, :], ALU.mult, [ptn, "TOPC"], ["SC"])
                    TT("dve", SC[:], SC[:], TOPC[:, :, 1, :], ALU.add, ["SC", "TOPC"], ["SC"])
                    for sub in range(4):
                        em.op("dve", lambda e, sub=sub: e.max(M8a[:], SC[:, sub, :]), ["SC"], ["M8a"])
                        em.op("dve", lambda e, sub=sub: e.match_replace(SC2[:, sub, :], M8a[:], SC[:, sub, :], -1e30), ["SC", "M8a"], ["SC2"])
                        em.op("dve", lambda e, sub=sub: e.max(M8b[:], SC2[:, sub, :]), ["SC2"], ["M8b"])
                        TSC("dve", SELB[:, sub, :], SC[:, sub, :], M8b[:, 7:8], NEG, ALU.is_lt, ALU.mult, ["SC", "M8b"], ["SELB"])
                    py, pyn = PS[6], PSN[6]
                    for sub in range(4):
                        TR(py[0:32, sub * 128:(sub + 1) * 128], SELB[:, sub, :], IDENT[:], ["SELB", "IDENT"], [pyn])
                    CP("dve", SELBT[64 * g:64 * g + 32, :], py[0:32, :], [pyn], ["SELBT%d" % g])
            add(topk)

            if mix_stop == "topk":
                return

            def score_tile(KT, kname, lo, hi, kt, qap, qname, g, sel, mask_idx):
                if mask_idx is None:
                    c0, c1 = 0, 512
                elif mask_idx < 4:
                    c0, c1 = 128 * mask_idx, 512
                else:
                    c0, c1 = 0, 128 * (mask_idx - 4 + 1)
                S_, sn = nextS()
                last = (not sel) and (mask_idx is None)
                MM(S_[:, c0:c1], KT[lo:hi, kt * 128:(kt + 1) * 128], qap[:, c0:c1], True, last, [kname, qname], [sn])
                if sel:
                    MM(S_[:, c0:c1], EMAT[64 * g:64 * g + 32, kt * 128:(kt + 1) * 128], SELBT[64 * g:64 * g + 32, c0:c1], False, mask_idx is None,
                       ["EMAT", "EMATb", "SELBT%d" % g], [sn])
                if mask_idx is not None:
                    MM(S_[:, c0:c1], IDENTB[:], CMWM[:, mask_idx, c0:c1], False, True, ["IDENTB", "CMWM"], [sn])
                P_, pn = nextP()
                ACT(P_[:, c0:c1], S_[:, c0:c1], AF.Exp, [sn], [pn], scale=0.125)
                return P_, pn, c0, c1

            def run_pipeline(steps, LA=2):
                pend = []
                for score_fn, post_fn in steps:
                    Pp = score_fn()
                    pend.append((post_fn, Pp))
                    if len(pend) > LA:
                        f_, a_ = pend.pop(0)
                        f_(*a_)
                for f_, a_ in pend:
                    f_(*a_)

            for gq in range(4):
                def f(W, gq=gq):
                    Wv = wview(W)
                    steps = []
                    for q in range(2):
                        j = 2 * gq + q
                        tgs, tgw = (4, 5) if j % 2 == 0 else (0, 1)
                        for g in range(2):
                            lo, hi = 64 * g, 64 * g + 64
                            qap = R1[lo:hi, 8 + j, :]
                            qn = r1(8 + j)
                            par = (j * 2 + g) % 2
                            os_, osn = PS[3 + 2 * par], PSN[3 + 2 * par]
                            ow_, own = PS[4 + 2 * par], PSN[4 + 2 * par]
                            kts = list(range(0, 4 * i + 4))
                            ktw = list(range(max(0, 4 * i - 4), 4 * i + 4))
                            if i > 0:
                                ktw = [4 * i - 1] + [k_ for k_ in ktw if k_ != 4 * i - 1]
                            first = True
                            for kt in kts:
                                r = kt - 4 * i

                                def sc(kt=kt, r=r, lo=lo, hi=hi, qap=qap, qn=qn, g=g, first=first, q=q, tgs=tgs, tgw=tgw):
                                    if first and g == 0:
                                        gate_tile(Wv, 2 * q, tgs)
                                        gate_tile(Wv, 2 * q + 1, tgw)
                                    return score_tile(KsT, "KsT%d" % (kt // 4), lo, hi, kt, qap, qn, g, i >= 2, r if r >= 0 else None)

                                def po(P_, pn, c0, c1, kt=kt, g=g, os_=os_, osn=osn, kts=kts):
                                    MM(os_[:, c0:c1], VsA[:, kt, g, :], P_[:, c0:c1], kt == kts[0], kt == kts[-1], ["VsA%d" % kt, "VsA1", pn], [osn])
                                steps.append((sc, po))
                                first = False
                            for kt in ktw:
                                r = kt - 4 * i
                                midx = r if r >= 0 else 4 + (kt - (4 * i - 4))

                                def sc(kt=kt, midx=midx, lo=lo, hi=hi, qap=qap, qn=qn, g=g):
                                    return score_tile(KwT, "KwT%d" % (kt // 4), lo, hi, kt, qap, qn, g, False, midx)

                                def po(P_, pn, c0, c1, kt=kt, g=g, j=j, lo=lo, hi=hi, os_=os_, osn=osn, ow_=ow_, own=own, ktw=ktw, tgs=tgs, tgw=tgw):
                                    MM(ow_[:, c0:c1], VwA[:, kt, g, :], P_[:, c0:c1], kt == ktw[0], kt == ktw[-1], ["VwA%d" % kt, "VwA1", pn], [own])
                                    if kt != ktw[-1]:
                                        return
                                    n6, n7, n8, n9 = "T6_%d" % g, "T7_%d" % g, "T8_%d" % g, "T9_%d" % g
                                    RECIP(TT_[6][lo:hi, :], os_[64:128, :], [osn], [n6])
                                    TT("dve", TT_[6][lo:hi, :], TT_[6][lo:hi, :], TT_[tgs][lo:hi, :], ALU.mult, [n6, "T%d" % tgs], [n6])
                                    TT("dve", TT_[8][lo:hi, :], os_[0:64, :], TT_[6][lo:hi, :], ALU.mult, [osn, n6], [n8])
                                    RECIP(TT_[7][lo:hi, :], ow_[64:128, :], [own], [n7])
                                    TT("dve", TT_[7][lo:hi, :], TT_[7][lo:hi, :], TT_[tgw][lo:hi, :], ALU.mult, [n7, "T%d" % tgw], [n7])
                                    TT("dve", TT_[9][lo:hi, :], ow_[0:64, :], TT_[7][lo:hi, :], ALU.mult, [own, n7], [n9])
                                    TT("pool", TT_[8][lo:hi, :], TT_[8][lo:hi, :], TT_[9][lo:hi, :], ALU.add, [n8, n9], [n8])
                                    TT("pool", R2[lo:hi, 8 + j, :], TT_[8][lo:hi, :], R2[lo:hi, j, :], ALU.add, [n8, r2(j)], [r2(8 + j)])
                                steps.append((sc, po))
                    run_pipeline(steps)
                add_w("wf", 16 + gq, 4096, f)

            if mix_stop == "slc":
                return

            def da_load(h):
                sl = h % 2
                ncol = TS_ * (i + 1)
                DMA("sp", KD[sl][:, 0:ncol], kda_d[h, :, 0:ncol], ["KDAD%d" % q for q in range(i + 1)], ["KD%d" % sl], "KD%d" % sl)
                DMA("sp", VD[sl][:, 0:4 * (i + 1), :], vda_d[h, :, 0:4 * (i + 1), :],
                    ["VDAD%d_%d" % (q, sub) for q in range(i + 1) for sub in range(4)], ["VD%d" % sl], "VD%d" % sl)

            def da_all():
                da_load(0)
                steps = []
                for h in range(8):
                    sl = h % 2
                    kts = list(range(0, 4 * i + 4))
                    for c in range(2):
                        lo, hi = 64 * c, 64 * c + 64
                        od, odn = PS[3 + 2 * c], PSN[3 + 2 * c]
                        sd, sdn = PS[4 + 2 * c], PSN[4 + 2 * c]
                        for kt in kts:
                            r = kt - 4 * i

                            def sc(kt=kt, r=r, lo=lo, hi=hi, h=h, sl=sl):
                                return score_tile(KD[sl], "KD%d" % sl, lo, hi, kt, R1[lo:hi, h, :], r1(h), 0, False, r if r >= 0 else None)

                            def po(P_, pn, c0, c1, kt=kt, c=c, h=h, sl=sl, od=od, odn=odn, sd=sd, sdn=sdn, kts=kts):
                                if c == 0 and kt == 0 and h + 1 < 8:
                                    da_load(h + 1)
                                MM(od[:, c0:c1], VD[sl][:, kt, :], P_[:, c0:c1], kt == 0, kt == kts[-1], ["VD%d" % sl, pn], [odn])
                                MM(sd[:, c0:c1], ONESB[:], P_[:, c0:c1], kt == 0, kt == kts[-1], ["ONESB", pn], [sdn])
                                if not (c == 1 and kt == kts[-1]):
                                    return
                                RECIP(TT_[0][:], PS[4][:], [PSN[4]], ["T0"])
                                TT("dve", TT_[0][:], PS[3][:], TT_[0][:], ALU.mult, [PSN[3], "T0"], ["T0"])
                                RECIP(TT_[1][:], PS[6][:], [PSN[6]], ["T1"])
                                TT("dve", TT_[1][:], PS[5][:], TT_[1][:], ALU.mult, [PSN[5], "T1"], ["T1"])
                                STT("dve", TT_[2][:], TT_[1][:], NEGLAM, TT_[0][:], ALU.mult, ALU.add, ["T1", "T0", "LT"], ["T2"])
                                Pq, pqn = nextP()
                                ACT(Pq[:], TT_[2][:], AF.Square, ["T2"], [pqn])
                                MM(PS[7][:], ONESB[:], Pq[:], True, True, ["ONESB", pqn], [PSN[7]])
                                ACT(TT_[3][:], PS[7][:], AF.Sqrt, [PSN[7]], ["T3"], bias=EPS, scale=1.0 / 128.0)
                                RECIP(TT_[3][:], TT_[3][:], ["T3"], ["T3"])
                                STT("dve", R1[:, 16 + h, :], TT_[2][:], HG[:, h:h + 1], TT_[3][:], ALU.mult, ALU.mult, ["T2", "HG", "T3"], [r1(16 + h)])
                            steps.append((sc, po))
                run_pipeline(steps)
            add(da_all)

            if mix_stop == "da":
                return

            for m in range(8):
                def f(W, m=m):
                    Wv = wview(W)
                    for k in range(8):
                        MM(PS[0][:], Wv[:, 0, k, :], R1[:, 16 + k, :], k == 0, k == 7, [Wv.name_, r1(16 + k)], [PSN[0]])
                    for k in range(8):
                        MM(PS[1][:], Wv[:, 1, k, :], R2[:, 8 + k, :], k == 0, k == 7, [Wv.name_, r2(8 + k)], [PSN[1]])
                    for k in range(8):
                        MM(PS[2][:], Wv[:, 2, k, :], HT[:, k, :], k == 0, k == 7, [Wv.name_, "HT%d" % k], [PSN[2]])
                    for k in range(8):
                        MM(PS[3][:], Wv[:, 3, k, :], HT[:, k, :], k == 0, k == 7, [Wv.name_, "HT%d" % k], [PSN[3]])
                    ACT(TT_[0][:], PS[2][:], AF.Sigmoid, [PSN[2]], ["T0"])
                    ACT(TT_[1][:], PS[3][:], AF.Sigmoid, [PSN[3]], ["T1"])
                    TT("dve", TT_[0][:], PS[0][:], TT_[0][:], ALU.mult, [PSN[0], "T0"], ["T0"])
                    TT("dve", TT_[1][:], PS[1][:], TT_[1][:], ALU.mult, [PSN[1], "T1"], ["T1"])
                    TT("pool", R1[:, 8 + m, :], TT_[0][:], TT_[1][:], ALU.add, ["T0", "T1"], [r1(8 + m)])
                add_w("wm", m, 4096, f)
            for half in range(2):
                def f(W, half=half):
                    Wv = W[:, 0:4096].rearrange("p (k c) -> p k c", k=8)
                    for ml in range(4):
                        m = 4 * half + ml
                        py, pyn = PS[4 + ml % 2], PSN[4 + ml % 2]
                        for k in range(8):
                            MM(py[:], Wv[:, k, ml * 128:(ml + 1) * 128], R1[:, 8 + k, :], k == 0, k == 7, [W.name_, r1(8 + k)], [pyn])
                        TT("dve", XT[:, m, :], py[:], XT[:, m, :], ALU.add, [pyn, "XT%d" % m], ["XT%d" % m])
                add_w("wo", half, 4096, f)

        def load_x(s, i):
            def fn():
                c0 = i * TS_
                DMA("sp", XIN[:], x_d[s, c0:c0 + TS_, :].rearrange("(a p) f -> p a f", p=128), (), ["XIN"], "XIN")
                for c in range(8):
                    pp, ppn = PS[c % 2], PSN[c % 2]
                    for sub in range(4):
                        TR(pp[:, sub * 128:(sub + 1) * 128], XIN[:, sub, c * 128:(c + 1) * 128], IDENT[:], ["XIN", "IDENT"], [ppn])
                    CP("act" if c % 2 else "dve", XT[:, c, :], pp[:], [ppn], ["XT%d" % c])
            add(fn)

        def store_out(s, i):
            def fn():
                c0 = i * TS_
                for c in range(8):
                    P_, pn = nextP()
                    ACT(P_[:], XT[:, c, :], AF.Square, ["XT%d" % c], [pn])
                    MM(PS[6][:], ONESB[:], P_[:], c == 0, c == 7, [pn, "ONESB"], [PSN[6]])
                ACT(TT_[9][:], PS[6][:], AF.Sqrt, [PSN[6]], ["T9_0", "T9_1"], bias=EPS, scale=1.0 / D)
                RECIP(TT_[9][:], TT_[9][:], ["T9_0", "T9_1"], ["T9_0", "T9_1"])
                for c in range(8):
                    STT("dve", XT[:, c, :], XT[:, c, :], G4[:, 3, c:c + 1], TT_[9][:], ALU.mult, ALU.mult,
                        ["XT%d" % c, "G4", "T9_0", "T9_1"], ["XT%d" % c])
                for sub in range(4):
                    for cc in range(2):
                        pp, ppn = PS[cc], PSN[cc]
                        for cl in range(4):
                            c = 4 * cc + cl
                            TR(pp[:, cl * 128:(cl + 1) * 128], XT[:, c, sub * 128:(sub + 1) * 128], IDENT[:], ["XT%d" % c, "IDENT"], [ppn])
                        CP("act" if cc else "dve", XIN[:, sub, cc * 512:(cc + 1) * 512], pp[:], [ppn], ["XIN"])
                DMA("sp", out_d[s, c0:c0 + TS_, :].rearrange("(a p) f -> p a f", p=128), XIN[:], ["XIN"], ["OUT"], "OUT")
            add(fn)

        for s in range(NSEQ):
            for i in range(NT):
                load_x(s, i)
                if stop_after != "load":
                    ffn_items("ffn1a", "ffn1b", 0)
                    if stop_after != "ffn1":
                        mixer_items(s, i)
                        mixc["on"] = False
                        if stop_after != "mix":
                            ffn_items("ffn2a", "ffn2b", 2)
                store_out(s, i)

        widx = [k for k, it in enumerate(items) if it[0] is not None]
        state = {"next": 0}

        def issue_upto(n):
            while state["next"] <= n and state["next"] < len(widx):
                q = state["next"]
                key, ci, nelem, _ = items[widx[q]]
                slot = q % 3
                DMA("sp", WR[slot][:, 0:nelem], wdst[key][ci][:, 0:nelem], ["S_" + key], ["WR%d" % slot], "WR%d" % slot)
                state["next"] += 1

        wq = 0
        for k, it in enumerate(items):
            key, ci, nelem, fn = it
            if key is None:
                fn()
            else:
                issue_upto(wq + 2)
                slot = wq % 3
                fn(_WProxy(WR[slot], "WR%d" % slot))
                wq += 1

        em.wait_all("sp", ["OUT"])
        em.emit()
    return nc


class _WProxy:
    def __init__(self, t, name):
        self.t = t
        self.name_ = name

    def __getitem__(self, key):
        return _APProxy(self.t[key], self.name_)


class _APProxy:
    def __init__(self, ap, name):
        self.ap = ap
        self.name_ = name

    def rearrange(self, *a, **k):
        return _APProxy(self.ap.rearrange(*a, **k), self.name_)

    def __getitem__(self, key):
        return self.ap[key]


_CACHE = {}


def kernel(**inputs):
    x = np.asarray(inputs["x"], np.float32)
    B = x.shape[0]
    nseq = B // N_CORES
    w = prep_weights(inputs)
    c = make_consts()
    key = ("prog", nseq)
    if key not in _CACHE:
        _CACHE[key] = build_program(NSEQ=nseq, NT=4)
    nc = _CACHE[key]
    shared = {}
    shared.update(w)
    shared.update(c)
    in_maps = []
    for core in range(N_CORES):
        m = dict(shared)
        m["x"] = np.ascontiguousarray(x[core * nseq:(core + 1) * nseq])
        in_maps.append(m)
    res = run_bass_kernel_spmd(nc, in_maps, core_ids=list(range(N_CORES)))
    out = np.concatenate([r["out"] for r in res.results], axis=0)
    return out.astype(np.float32)
```
